# Optimizing a Trainium2 kernel written in Bass

```python
import math
import jax, jax.numpy as jnp
from jax import lax
import numpy as np

D_MODEL = 1024
BATCH = 8
SEQ = 2048
DEPTH = 4
DEC_BATCH = 128
DEC_SEQ = 1
PAST_LEN = 16384
PAGE_SIZE = 128

N_AB = (DEPTH + 1) // 2
N_C = DEPTH // 2
CONV_W = 4
W_A = D_MODEL
H_A = 8
BW_A = W_A // H_A
LRU_C = 8.0
H_B = 8
DK = 128
DV = 128
W_B = H_B * DV
QKV_B = 2 * H_B * DK + H_B * DV
CHUNK = 64
IN_AB = 2 * W_A + QKV_B + W_B + 2 * H_B
OUT_AB = W_A + W_B
W_C = D_MODEL
CG = 16
G_C = W_C // CG
P_C = 64
EPS = 1e-6

kernel_name = "hybrid_rglru_gdn_s5_step"


def rms_norm(x, w):
    x32 = x.astype(jnp.float32)
    y = x32 * lax.rsqrt(jnp.mean(x32 * x32, axis=-1, keepdims=True) + EPS)
    return (y * w.astype(jnp.float32)).astype(x.dtype)


def l2_norm(t):
    return t * lax.rsqrt(jnp.sum(t * t, axis=-1, keepdims=True) + EPS)


def causal_conv(x, buf, w):
    L = x.shape[1]
    xp = jnp.concatenate([buf.astype(x.dtype), x], axis=1)
    y = sum(xp[:, j:j + L] * w[j] for j in range(CONV_W))
    return y, xp[:, L:]


def linear_combine(e1, e2):
    a1, b1 = e1
    a2, b2 = e2
    return a1 * a2, a2 * b1 + b2


def rg_lru(x, gx_w, gx_b, ga_w, ga_b, a_param, h0, start_pos):
    B, L, _ = x.shape
    xb = x.reshape(B, L, H_A, BW_A)
    gate_x = jax.nn.sigmoid(jnp.einsum("blhi,hij->blhj", xb, gx_w).reshape(B, L, W_A) + gx_b)
    gate_a = jax.nn.sigmoid(jnp.einsum("blhi,hij->blhj", xb, ga_w).reshape(B, L, W_A) + ga_b)
    log_a = -LRU_C * gate_a * jax.nn.softplus(-a_param)
    a = jnp.exp(log_a)
    mult = jnp.sqrt(-jnp.expm1(2.0 * log_a))
    if start_pos == 0:
        mult = mult.at[:, 0].set(1.0)
    b = mult * gate_x * x
    b = b.at[:, 0].add(a[:, 0] * h0)
    _, h = lax.associative_scan(linear_combine, (a, b), axis=1)
    return h, h[:, -1]


def gated_delta_rule(q, k, v, g, beta, s0):
    B, L, H, _ = q.shape
    n = -(-L // CHUNK)
    pad = n * CHUNK - L

    def blocks(t):
        t = jnp.pad(t, [(0, 0), (0, pad)] + [(0, 0)] * (t.ndim - 2))
        t = t.reshape((B, n, CHUNK) + t.shape[2:])
        return jnp.moveaxis(t, (1, 2), (0, 3))

    qb, kb, vb, gb, bb = blocks(q), blocks(k), blocks(v), blocks(g), blocks(beta)
    gc = jnp.cumsum(gb, axis=-1)
    idx = jnp.arange(CHUNK)
    tri = idx[:, None] >= idx[None, :]
    strict = idx[:, None] > idx[None, :]
    decay = jnp.exp(jnp.where(tri, gc[..., :, None] - gc[..., None, :], -jnp.inf))
    k_beta = kb * bb[..., None]
    A = jnp.where(strict, jnp.einsum("nbhid,nbhjd->nbhij", k_beta, kb) * decay, 0.0)
    eye = jnp.eye(CHUNK, dtype=A.dtype)
    rhs = jnp.concatenate([vb * bb[..., None], k_beta * jnp.exp(gc)[..., None]], axis=-1)
    sol = lax.linalg.triangular_solve(A + eye, rhs, left_side=True, lower=True)
    u, w = sol[..., :DV], sol[..., DV:]
    qk = jnp.where(tri, jnp.einsum("nbhid,nbhjd->nbhij", qb, kb) * decay, 0.0)

    def step(S, inp):
        qc, kc, uc, wc, gcc, qkc = inp
        v_new = uc - jnp.einsum("bhck,bhkv->bhcv", wc, S)
        o = (jnp.einsum("bhck,bhkv->bhcv", qc * jnp.exp(gcc)[..., None], S)
             + jnp.einsum("bhij,bhjv->bhiv", qkc, v_new))
        g_last = gcc[..., -1:]
        S = (S * jnp.exp(g_last)[..., None]
             + jnp.einsum("bhck,bhcv->bhkv", kc * jnp.exp(g_last - gcc)[..., None], v_new))
        return S, o

    S, o = lax.scan(step, s0, (qb, kb, u, w, gc, qk))
    o = jnp.moveaxis(o, (0, 3), (1, 2)).reshape(B, n * CHUNK, H, DV)[:, :L]
    return o, S


def ab_mixer(h, conv_a, lru_h, conv_b, delta_s, start_pos, in_w, out_w, conv_a_w, conv_a_b,
             gx_w, gx_b, ga_w, ga_b, a_param, conv_b_w, a_log, dt_bias, gnorm_w):
    B, L, _ = h.shape
    f32 = jnp.float32
    proj = jnp.dot(h, in_w)
    cuts = np.cumsum([W_A, W_A, QKV_B, W_B, H_B]).tolist()
    xa, za, qkv, zb, b_raw, a_raw = jnp.split(proj, cuts, axis=-1)
    xa_c, new_conv_a = causal_conv(xa, conv_a, conv_a_w)
    xa_c = (xa_c + conv_a_b).astype(f32)
    ya, new_lru = rg_lru(xa_c, gx_w.astype(f32), gx_b.astype(f32), ga_w.astype(f32),
                         ga_b.astype(f32), a_param.astype(f32), lru_h.astype(f32), start_pos)
    ya = ya * jax.nn.silu(za.astype(f32))
    qkv_c, new_conv_b = causal_conv(qkv, conv_b, conv_b_w)
    qkv_c = jax.nn.silu(qkv_c.astype(f32))
    q, k, v = jnp.split(qkv_c, [H_B * DK, 2 * H_B * DK], axis=-1)
    q = l2_norm(q.reshape(B, L, H_B, DK)) * (DK ** -0.5)
    k = l2_norm(k.reshape(B, L, H_B, DK))
    v = v.reshape(B, L, H_B, DV)
    beta = jax.nn.sigmoid(b_raw.astype(f32))
    g = -jnp.exp(a_log.astype(f32)) * jax.nn.softplus(a_raw.astype(f32) + dt_bias.astype(f32))
    o, new_delta = gated_delta_rule(q, k, v, g, beta, delta_s.astype(f32))
    o = rms_norm(o, gnorm_w) * jax.nn.silu(zb.astype(f32).reshape(B, L, H_B, DV))
    y = jnp.concatenate([ya, o.reshape(B, L, W_B)], axis=-1).astype(h.dtype)
    return jnp.dot(y, out_w), new_conv_a, new_lru, new_conv_b, new_delta


def s5_mixer(h, s_re, s_im, in_w, out_w, a_re, a_im, b_re, b_im, c_re, c_im, d, log_dt,
             glu_w, glu_b):
    B, L, _ = h.shape
    f32 = jnp.float32
    u, z = jnp.split(jnp.dot(h, in_w), 2, axis=-1)
    u32 = u.astype(f32)
    A = lax.complex(a_re.astype(f32), a_im.astype(f32))
    dt = jnp.exp(log_dt.astype(f32))[:, None]
    A_bar = jnp.exp(A * dt)
    B_bar = ((A_bar - 1.0) / A)[..., None] * lax.complex(b_re.astype(f32), b_im.astype(f32))
    Bu = jnp.einsum("gpc,blgc->lbgp", B_bar,
                    u32.reshape(B, L, G_C, CG).astype(jnp.complex64))
    x0 = lax.complex(s_re.astype(f32), s_im.astype(f32))
    Bu = Bu.at[0].add(A_bar * x0)
    a_el = jnp.broadcast_to(A_bar, (L, 1, G_C, P_C))
    _, xs = lax.associative_scan(linear_combine, (a_el, Bu), axis=0)
    C = lax.complex(c_re.astype(f32), c_im.astype(f32))
    y = jnp.einsum("gcp,lbgp->blgc", C, xs).real.reshape(B, L, W_C) + d.astype(f32) * u32
    y = jax.nn.gelu(y)
    y = y * jax.nn.sigmoid(jnp.dot(y, glu_w.astype(f32)) + glu_b.astype(f32))
    y = (y * jax.nn.silu(z.astype(f32))).astype(h.dtype)
    return jnp.dot(y, out_w), xs[-1].real, xs[-1].imag


def trunk(x, c, conv_a, lru_h, conv_b, delta_s, s5_re, s5_im, start_pos, w):
    n_ca, n_lru, n_cb, n_ds, n_sr, n_si = [], [], [], [], [], []
    for i in range(DEPTH):
        j = i // 2
        mod = jnp.dot(jax.nn.silu(c), w["mod_w"][i]) + w["mod_b"][i]
        shift, scale, gate = jnp.split(mod[:, None, :], 3, axis=-1)
        h = rms_norm(x, w["norm_w"][i]) * (1.0 + scale) + shift
        if i % 2 == 0:
            y, ca, lh, cb, ds = ab_mixer(
                h, conv_a[j], lru_h[j], conv_b[j], delta_s[j], start_pos,
                w["ab_in_w"][j], w["ab_out_w"][j], w["conv_a_w"][j], w["conv_a_b"][j],
                w["lru_gx_w"][j], w["lru_gx_b"][j], w["lru_ga_w"][j], w["lru_ga_b"][j],
                w["lru_a_param"][j], w["conv_b_w"][j], w["gdn_a_log"][j],
                w["gdn_dt_bias"][j], w["gdn_norm_w"][j])
            n_ca.append(ca.astype(conv_a.dtype))
            n_lru.append(lh.astype(lru_h.dtype))
            n_cb.append(cb.astype(conv_b.dtype))
            n_ds.append(ds.astype(delta_s.dtype))
        else:
            y, sr, si = s5_mixer(
                h, s5_re[j], s5_im[j], w["c_in_w"][j], w["c_out_w"][j],
                w["s5_a_re"][j], w["s5_a_im"][j], w["s5_b_re"][j], w["s5_b_im"][j],
                w["s5_c_re"][j], w["s5_c_im"][j], w["s5_d"][j], w["s5_log_dt"][j],
                w["glu_w"][j], w["glu_b"][j])
            n_sr.append(sr.astype(s5_re.dtype))
            n_si.append(si.astype(s5_im.dtype))
        x = x + (gate * y).astype(x.dtype)
    y = rms_norm(x, w["final_norm_w"])
    return (y, jnp.stack(n_ca), jnp.stack(n_lru), jnp.stack(n_cb), jnp.stack(n_ds),
            jnp.stack(n_sr), jnp.stack(n_si))


def setup_inputs(seed: int = 0) -> dict:
    key = jax.random.key(seed)
    ks = list(jax.random.split(key, 48))
    f32 = jnp.float32

    def nrm(shape, s):
        return jax.random.normal(ks.pop(), shape, f32) * s

    def unif(shape, lo, hi):
        return jax.random.uniform(ks.pop(), shape, f32, lo, hi)

    a8 = unif((N_AB, W_A), 0.9, 0.999)
    base = a8 ** (1.0 / LRU_C)
    dt_b = jnp.exp(unif((N_AB, H_B), math.log(1e-3), math.log(1e-1)))
    return {
        "x_prompt": nrm((BATCH, SEQ, D_MODEL), 1.0),
        "x_sample": nrm((DEC_BATCH, DEC_SEQ, D_MODEL), 1.0),
        "c_prompt": nrm((BATCH, D_MODEL), 1.0),
        "c_sample": nrm((DEC_BATCH, D_MODEL), 1.0),
        "state_conv_a": nrm((N_AB, DEC_BATCH, CONV_W - 1, W_A), 1.0),
        "state_lru": nrm((N_AB, DEC_BATCH, W_A), 0.5),
        "state_conv_b": nrm((N_AB, DEC_BATCH, CONV_W - 1, QKV_B), 1.0),
        "state_delta": nrm((N_AB, DEC_BATCH, H_B, DK, DV), 0.1),
        "state_s5_re": nrm((N_C, DEC_BATCH, G_C, P_C), 0.1),
        "state_s5_im": nrm((N_C, DEC_BATCH, G_C, P_C), 0.1),
        "norm_w": 1.0 + nrm((DEPTH, D_MODEL), 0.01),
        "mod_w": nrm((DEPTH, D_MODEL, 3 * D_MODEL), D_MODEL ** -0.5),
        "mod_b": nrm((DEPTH, 3 * D_MODEL), 0.01),
        "ab_in_w": nrm((N_AB, D_MODEL, IN_AB), D_MODEL ** -0.5),
        "ab_out_w": nrm((N_AB, OUT_AB, D_MODEL), OUT_AB ** -0.5),
        "conv_a_w": nrm((N_AB, CONV_W, W_A), CONV_W ** -0.5),
        "conv_a_b": nrm((N_AB, W_A), 0.01),
        "lru_gx_w": nrm((N_AB, H_A, BW_A, BW_A), BW_A ** -0.5),
        "lru_gx_b": nrm((N_AB, W_A), 0.01),
        "lru_ga_w": nrm((N_AB, H_A, BW_A, BW_A), BW_A ** -0.5),
        "lru_ga_b": nrm((N_AB, W_A), 0.01),
        "lru_a_param": jnp.log(base) - jnp.log1p(-base),
        "conv_b_w": nrm((N_AB, CONV_W, QKV_B), CONV_W ** -0.5),
        "gdn_a_log": jnp.log(unif((N_AB, H_B), 1.0, 16.0)),
        "gdn_dt_bias": dt_b + jnp.log(-jnp.expm1(-dt_b)),
        "gdn_norm_w": 1.0 + nrm((N_AB, DV), 0.01),
        "c_in_w": nrm((N_C, D_MODEL, 2 * W_C), D_MODEL ** -0.5),
        "c_out_w": nrm((N_C, W_C, D_MODEL), W_C ** -0.5),
        "s5_a_re": -0.5 + nrm((N_C, G_C, P_C), 0.01),
        "s5_a_im": jnp.pi * jnp.arange(P_C, dtype=f32) + nrm((N_C, G_C, P_C), 0.01),
        "s5_b_re": nrm((N_C, G_C, P_C, CG), (2 * CG) ** -0.5),
        "s5_b_im": nrm((N_C, G_C, P_C, CG), (2 * CG) ** -0.5),
        "s5_c_re": nrm((N_C, G_C, CG, P_C), P_C ** -0.5),
        "s5_c_im": nrm((N_C, G_C, CG, P_C), P_C ** -0.5),
        "s5_d": nrm((N_C, W_C), 1.0),
        "s5_log_dt": unif((N_C, G_C), math.log(1e-3), math.log(1e-1)),
        "glu_w": nrm((N_C, W_C, W_C), W_C ** -0.5),
        "glu_b": nrm((N_C, W_C), 0.01),
        "final_norm_w": 1.0 + nrm((D_MODEL,), 0.01),
    }


def reference(x_prompt, x_sample, c_prompt, c_sample, state_conv_a, state_lru, state_conv_b,
              state_delta, state_s5_re, state_s5_im, norm_w, mod_w, mod_b, ab_in_w, ab_out_w,
              conv_a_w, conv_a_b, lru_gx_w, lru_gx_b, lru_ga_w, lru_ga_b, lru_a_param,
              conv_b_w, gdn_a_log, gdn_dt_bias, gdn_norm_w, c_in_w, c_out_w, s5_a_re, s5_a_im,
              s5_b_re, s5_b_im, s5_c_re, s5_c_im, s5_d, s5_log_dt, glu_w, glu_b, final_norm_w):
    w = dict(norm_w=norm_w, mod_w=mod_w, mod_b=mod_b, ab_in_w=ab_in_w, ab_out_w=ab_out_w,
             conv_a_w=conv_a_w, conv_a_b=conv_a_b, lru_gx_w=lru_gx_w, lru_gx_b=lru_gx_b,
             lru_ga_w=lru_ga_w, lru_ga_b=lru_ga_b, lru_a_param=lru_a_param, conv_b_w=conv_b_w,
             gdn_a_log=gdn_a_log, gdn_dt_bias=gdn_dt_bias, gdn_norm_w=gdn_norm_w,
             c_in_w=c_in_w, c_out_w=c_out_w, s5_a_re=s5_a_re, s5_a_im=s5_a_im,
             s5_b_re=s5_b_re, s5_b_im=s5_b_im, s5_c_re=s5_c_re, s5_c_im=s5_c_im, s5_d=s5_d,
             s5_log_dt=s5_log_dt, glu_w=glu_w, glu_b=glu_b, final_norm_w=final_norm_w)
    nb = x_prompt.shape[0]
    z_conv_a = jnp.zeros((N_AB, nb, CONV_W - 1, W_A), state_conv_a.dtype)
    z_lru = jnp.zeros((N_AB, nb, W_A), state_lru.dtype)
    z_conv_b = jnp.zeros((N_AB, nb, CONV_W - 1, QKV_B), state_conv_b.dtype)
    z_delta = jnp.zeros((N_AB, nb, H_B, DK, DV), state_delta.dtype)
    z_s5_re = jnp.zeros((N_C, nb, G_C, P_C), state_s5_re.dtype)
    z_s5_im = jnp.zeros((N_C, nb, G_C, P_C), state_s5_im.dtype)
    y_prompt, p_conv_a, p_lru, p_conv_b, p_delta, p_s5_re, p_s5_im = trunk(
        x_prompt, c_prompt, z_conv_a, z_lru, z_conv_b, z_delta, z_s5_re, z_s5_im, 0, w)
    y_sample, s_conv_a, s_lru, s_conv_b, s_delta, s_s5_re, s_s5_im = trunk(
        x_sample, c_sample, state_conv_a, state_lru, state_conv_b, state_delta,
        state_s5_re, state_s5_im, PAST_LEN, w)
    return (y_prompt, y_sample, p_conv_a, p_lru, p_conv_b, p_delta, p_s5_re, p_s5_im,
            s_conv_a, s_lru, s_conv_b, s_delta, s_s5_re, s_s5_im)
```

```python
from contextlib import ExitStack
import numpy as np
import concourse.bass as bass
import concourse.mybir as mybir
from concourse.bass_utils import run_bass_kernel_spmd

F32 = mybir.dt.float32
BF16 = mybir.dt.bfloat16
AF = mybir.ActivationFunctionType
ALU = mybir.AluOpType


class _RS:
    __slots__ = ("w", "r")

    def __init__(self):
        self.w = None
        self.r = {}


class T:
    def __init__(self, h, name):
        self.h = h
        self.name = name
        self.subs = {None: _RS()}

    def __getitem__(self, idx):
        return self.h[idx]


class Sched:
    def __init__(self, nc, n_sp=12, n_pool=8, n_act=6):
        self.nc = nc
        self.es = ExitStack()
        self.eng = {"pe": nc.tensor, "dve": nc.vector, "act": nc.scalar, "pool": nc.gpsimd, "sp": nc.sync}
        self.sem = {}
        self.cnt = {}
        for e in self.eng:
            self.sem[e] = self.es.enter_context(nc.semaphore("s_" + e))
            self.cnt[e] = 0
        self.dq = {}
        for q, n in (("sp", n_sp), ("pool", n_pool), ("act", n_act)):
            sems = [self.es.enter_context(nc.semaphore("d_%s%d" % (q, i))) for i in range(n)]
            self.dq[q] = {"sems": sems, "n": 0}
        self.semobj = {}
        for e in self.eng:
            self.semobj[("e", e)] = self.sem[e]
        for q in self.dq:
            for i, s in enumerate(self.dq[q]["sems"]):
                self.semobj[("d", q, i)] = s
        self.waited = {e: {} for e in self.eng}
        self.out_tokens = []
        self.ninst = 0
        self.marks = []

    def sb(self, name, shape, dtype=F32):
        h = self.es.enter_context(self.nc.sbuf_tensor(name, list(shape), dtype))
        return T(h, name)

    def ps(self, name, shape, dtype=F32):
        h = self.es.enter_context(self.nc.psum_tensor(name, list(shape), dtype))
        return T(h, name)

    @staticmethod
    def _res(x):
        if isinstance(x, T):
            return x, None
        if hasattr(x, "res"):
            return x.res
        return x[0], x[1]

    def _collect(self, r, w):
        deps = {}

        def add(tok):
            if tok is None:
                return
            k, v = tok
            if deps.get(k, 0) < v:
                deps[k] = v

        for x in r:
            t, k = self._res(x)
            if k is None:
                for rs in t.subs.values():
                    add(rs.w)
            else:
                add(t.subs[None].w)
                if k in t.subs:
                    add(t.subs[k].w)
        for x in w:
            t, k = self._res(x)
            if k is None:
                lst = list(t.subs.values())
            else:
                lst = [t.subs[None]]
                if k in t.subs:
                    lst.append(t.subs[k])
            for rs in lst:
                add(rs.w)
                for kk, vv in rs.r.items():
                    add((kk, vv))
        return deps

    def _commit(self, r, w, tok):
        k0, v0 = tok
        for x in r:
            t, k = self._res(x)
            rs = t.subs.setdefault(k, _RS())
            if rs.r.get(k0, 0) < v0:
                rs.r[k0] = v0
        for x in w:
            t, k = self._res(x)
            if k is None:
                t.subs = {None: _RS()}
                t.subs[None].w = tok
            else:
                rs = t.subs.setdefault(k, _RS())
                rs.w = tok
                rs.r = {}

    def _waits(self, e, deps, skip_self=False):
        wd = self.waited[e]
        for k, v in deps.items():
            if skip_self and k == ("e", e):
                continue
            if wd.get(k, 0) >= v:
                continue
            self.eng[e].wait_ge(self.semobj[k], v)
            wd[k] = v

    def op(self, e, fn, r=(), w=()):
        deps = self._collect(r, w)
        self._waits(e, deps, skip_self=(e == "pe"))
        ins = fn(self.eng[e])
        self.cnt[e] += 1
        ins.then_inc(self.sem[e], 1)
        tok = (("e", e), self.cnt[e])
        self.waited[e][("e", e)] = max(self.waited[e].get(("e", e), 0), 0)
        self._commit(r, w, tok)
        self.ninst += 1
        return tok

    def dma(self, out, in_, r=(), w=(), cast=False, q=None, is_out=False, **kw):
        if q is None:
            q = "pool" if cast else "sp"
        dq = self.dq[q]
        n = dq["n"]
        dq["n"] += 1
        P = len(dq["sems"])
        i = n % P
        rnd = n // P
        deps = self._collect(r, w)
        if rnd > 0:
            k = ("d", q, i)
            if deps.get(k, 0) < 16 * rnd:
                deps[k] = 16 * rnd
        self._waits(q, deps)
        ins = self.eng[q].dma_start(out=out, in_=in_, **kw)
        ins.then_inc(dq["sems"][i], 16)
        tok = (("d", q, i), 16 * (rnd + 1))
        self._commit(r, w, tok)
        if is_out:
            self.out_tokens.append(tok)
        self.ninst += 1
        return tok

    def finish(self):
        deps = {}
        for k, v in self.out_tokens:
            if deps.get(k, 0) < v:
                deps[k] = v
        self._waits("sp", deps)
        self.es.close()


TT = 1024
NSEG = 2048 // TT
NCH = TT // 128
TW = TT + 16
EPS = 1e-6
IN_AB = 6160
TWO_PI = 6.283185307179586


def sp_layout():
    items = [("mod_b", 96), ("norm_w", 32), ("fnorm_w", 8), ("conv_a_w", 64), ("conv_a_b", 16),
             ("gx_b", 16), ("ga_b", 16), ("a_param", 16), ("conv_b_w", 192), ("gnorm_w", 2),
             ("a_log", 144), ("dt_bias", 144), ("s5_are", 64), ("s5_aim", 64), ("s5_ldt", 64),
             ("s5_d", 16), ("glu_b", 16)]
    lay = {}
    off = 0
    for n, w in items:
        lay[n] = (off, w)
        off += w
    return lay, off


def build_program():
    nc = bass.Bass("TRN2", target_bir_lowering=False)
    lay, SPN = sp_layout()

    def din(name, shape):
        return nc.dram_tensor(name, list(shape), F32, kind="ExternalInput").ap()

    def dout(name, shape):
        return nc.dram_tensor(name, list(shape), F32, kind="ExternalOutput").ap()

    d_xp = din("xp", [128, 8, 2048]); d_xs = din("xs", [128, 8, 16]); d_cT = din("cT", [128, 8, 17])
    d_sp = din("sp", [128, SPN]); d_modw = din("modw", [4, 128, 8, 3072])
    d_abin = din("abin", [2, 128, 8, IN_AB]); d_about = din("about", [2, 128, 16, 1024])
    d_gxw = din("gxw", [2, 128, 8, 128]); d_gaw = din("gaw", [2, 128, 8, 128])
    d_cin = din("cin", [2, 128, 8, 2048]); d_cout = din("cout", [2, 128, 8, 1024])
    d_gluw = din("gluw", [2, 128, 8, 1024])
    d_s5B = din("s5B", [2, 2, 32, 128, 128]); d_s5C = din("s5C", [2, 2, 32, 128, 128])
    d_sca = din("sca", [128, 2, 8, 16, 3]); d_slru = din("slru", [128, 2, 8, 16])
    d_scb = din("scb", [128, 2, 24, 16, 3]); d_sdelta = din("sdelta", [2, 16, 8, 128, 128])
    d_mk = din("mk", [128, 4, 128])
    d_ss5re = din("ss5re", [128, 2, 32, 16]); d_ss5im = din("ss5im", [128, 2, 32, 16])
    o_y = dout("o_y", [128, 8, 2064]); o_ca = dout("o_ca", [128, 2, 8, 17, 3])
    o_lru = dout("o_lru", [128, 2, 8, 17]); o_cb = dout("o_cb", [128, 2, 24, 17, 3])
    o_delta = dout("o_delta", [2, 17, 8, 128, 128])
    o_s5re = dout("o_s5re", [128, 2, 32, 16]); o_s5im = dout("o_s5im", [128, 2, 32, 16])
    o_s5pre = dout("o_s5pre", [128, 2, 32]); o_s5pim = dout("o_s5pim", [128, 2, 32])

    S = Sched(nc)
    sb, op, dma = S.sb, S.op, S.dma

    def mm(out, lhsT, rhs, start, stop, r, w):
        op("pe", lambda e: e.matmul(out, lhsT=lhsT, rhs=rhs, start=start, stop=stop), r=r, w=w)

    def A(out, in_, func, r, w, bias=None, scale=None, eng="act"):
        kw = {}
        if bias is not None:
            kw["bias"] = bias
        if scale is not None:
            kw["scale"] = scale
        op("act", lambda e: e.activation(out=out, in_=in_, func=func, **kw), r=r, w=w)

    def tt(eng, out, a, b, o, r, w):
        op(eng, lambda e: e.tensor_tensor(out=out, in0=a, in1=b, op=o), r=r, w=w)

    def ts(eng, out, a, s1, o0, r, w, s2=None, o1=None):
        if o1 is None:
            op(eng, lambda e: e.tensor_scalar(out=out, in0=a, scalar1=s1, scalar2=None, op0=o0), r=r, w=w)
        else:
            op(eng, lambda e: e.tensor_scalar(out=out, in0=a, scalar1=s1, scalar2=s2, op0=o0, op1=o1), r=r, w=w)

    def stt(out, a, s, b, o0, o1, r, w):
        op("dve", lambda e: e.scalar_tensor_tensor(out=out, in0=a, scalar=s, in1=b, op0=o0, op1=o1), r=r, w=w)

    def cp(eng, out, in_, r, w):
        if eng == "act":
            op("act", lambda e: e.copy(out=out, in_=in_), r=r, w=w)
        else:
            op(eng, lambda e: e.tensor_copy(out=out, in_=in_), r=r, w=w)

    MUL, ADD, SUB = ALU.mult, ALU.add, ALU.subtract

    X = sb("X", [128, 8, TW]); H = sb("H", [128, 8, TW], BF16)
    RS = sb("RS", [128, TW])
    SP = sb("SP", [128, SPN]); MODT = sb("MODT", [128, 4, 24, 17]); A1 = sb("A1", [128, 8, 17])
    SC = sb("SC", [128, 8, 17]); CT = sb("CT", [128, 8, 17])
    ident = sb("ident", [128, 128]); ones = sb("ones", [128, 128]); U = sb("U", [128, 128])
    MLS = sb("MLS", [128, 128]); ramp = sb("ramp", [128, TT])
    zeros = sb("zeros", [128, 128]); negones = sb("negones", [128, 1]); MK = sb("MK", [128, 4, 128])
    MW = [sb("MW%d" % i, [128, 8, 128]) for i in range(2)]
    Wt = [sb("Wt%d" % i, [128, 8, 128], BF16) for i in range(4)]
    OW = [sb("OW%d" % i, [128, 4, 128], BF16) for i in range(2)]
    GW = [sb("GW%d" % i, [128, 2, 128]) for i in range(2)]
    WK = [sb("WK%d" % i, [128, TW + 3]) for i in range(8)]
    YB = sb("YB", [128, 12, TW], BF16)
    XB = [sb("XB%d" % i, [128, TW], BF16) for i in range(3)]
    KI = sb("KI", [128, TT], mybir.dt.int32)
    SST = sb("SST", [128, 2, 8, 128]); STMP = sb("STMP", [128, 128])
    CCA = sb("CCA", [128, 2, 8, 3]); CCB = sb("CCB", [128, 2, 24, 3]); CLRU = sb("CLRU", [128, 2, 8])
    CS5 = sb("CS5", [128, 2, 2, 32]); CMID = sb("CMID", [128, 2, 2])
    C1 = sb("C1", [128, 2, 8]); C2 = sb("C2", [128, 2, 8])
    S5P = sb("S5P", [128, 12, 64])
    SMALL = [sb("SM%d" % i, [128, 16, 3]) for i in range(4)]
    LR0 = sb("LR0", [128, 8, 16])
    BG = sb("BG", [128, 9, 16]); BTM = sb("BTM", [128, 9, 8]); GTM = sb("GTM", [128, 9, 8])
    GT = [sb("GT%d" % i, [128, 72]) for i in range(6)]
    GCS = sb("GCS", [128, NCH, 16]); EGC = sb("EGC", [128, NCH, 8]); BEGE = sb("BEGE", [128, NCH, 8])
    EDEC = sb("EDEC", [128, NCH, 8]); EGL = sb("EGL", [128, NCH, 8])
    RM = sb("RM", [128, 2, 8, 16]); EGB = sb("EGB", [128, 2, 8, 16])
    CK = {n: sb("CK_" + n, [128, 128]) for n in
          ["gcr", "egr", "kbg", "kdec", "vb", "gbc", "E1", "E1T", "decL", "decU", "A0", "QKmT", "qg", "B0", "P0", "P1",
           "Aa", "Ab", "Ba", "Bb", "u", "wT", "vnew", "R0", "R1"]}
    CKX = [sb("CKX%d" % i, [128, 128]) for i in range(7)]
    S0 = []; SN = []
    DFM = sb("DFM", [128, 12])
    BP = [sb("BP%d" % i, [128, 2, 128], BF16) for i in range(2)]
    CR = [sb("CR%d" % i, [128, 2, 128]) for i in range(2)]
    CF = [sb("CF%d" % i, [128, 2, 128], BF16) for i in range(2)]
    CTMP = sb("CTMP", [128, 128])
    S5S = sb("S5S", [128, 2, 16]); S5N = sb("S5N", [128, 3, 16]); S5T = sb("S5T", [128, 2, 16]); S5U = sb("S5U", [128, 16]); S5O = sb("S5O", [128, 2, 32])
    PS = [S.ps("ps%d" % i, [128, 512]) for i in range(8)]
    psrot = {"lst": list(range(8)), "i": 0}

    class View:
        def __init__(self, ap, res):
            self.ap = ap
            self.res = res

        def __getitem__(self, idx):
            return self.ap[idx]

    def make_psn(banks):
        st = {"i": 0}

        def psn():
            p = PS[banks[st["i"] % len(banks)]]
            st["i"] += 1
            return p
        return psn

    PSN = [make_psn([2 * q_, 2 * q_ + 1]) for q_ in range(4)]

    def run_interleaved(gens):
        gens = list(gens)
        while gens:
            for g_ in list(gens):
                try:
                    next(g_)
                except StopIteration:
                    gens.remove(g_)

    ybf = YB[:, 4:12, :].bitcast(F32)
    hpool = list(CK.values())
    for i_ in range(32):
        hpool.append(View(ybf[:, i_ // 4, (i_ % 4) * 128:(i_ % 4 + 1) * 128], (YB, ("ck", i_))))
    for i_ in range(8):
        hpool.append(View(WK[7][:, i_ * 128:(i_ + 1) * 128], (WK[7], ("ck", i_))))
    hpool += CKX
    SNAMES = ["kbg", "kdec", "vb", "QKmT", "qg", "u", "wT", "vnew", "A0", "B0", "Aa", "Ab", "Ba", "Bb", "P0", "P1", "R0", "R1"]
    ALIAS = {"gbc": "P0", "gcr": "Aa", "E1": "Ab", "E1T": "Ba", "decL": "Bb", "decU": "P1", "egr": "R0"}
    NSTR = 4
    assert len(hpool) >= NSTR * len(SNAMES), len(hpool)
    CKS = []
    for q_ in range(NSTR):
        d_ = {n_: hpool[q_ * len(SNAMES) + i_] for i_, n_ in enumerate(SNAMES)}
        for a_, b_ in ALIAS.items():
            d_[a_] = d_[b_]
        CKS.append(d_)

    def nextps():
        l = psrot["lst"]
        p = PS[l[psrot["i"] % len(l)]]
        psrot["i"] += 1
        return p

    def spc(name, a=0, b=None):
        off, w = lay[name]
        if b is None:
            b = w
        return SP[:, off + a:off + b]

    def spcol(name, i):
        off, _ = lay[name]
        return SP[:, off + i:off + i + 1]

    dma(SP[:], d_sp, w=[SP]); dma(CT[:], d_cT, w=[CT]); dma(MK[:], d_mk, w=[MK])
    op("pool", lambda e: e.memset(ident[:], 0.0), w=[ident])
    op("pool", lambda e: e.affine_select(out=ident[:], in_=ident[:], pattern=[[-1, 128]], compare_op=ALU.not_equal,
                                         fill=1.0, base=0, channel_multiplier=1), r=[ident], w=[ident])
    op("pool", lambda e: e.memset(ones[:], 1.0), w=[ones])
    op("pool", lambda e: e.memset(zeros[:], 0.0), w=[zeros])
    op("pool", lambda e: e.memset(negones[:], -1.0), w=[negones])
    op("pool", lambda e: e.memset(U[:], 1.0), w=[U])
    op("pool", lambda e: e.affine_select(out=U[:], in_=U[:], pattern=[[1, 128]], compare_op=ALU.is_ge,
                                         fill=0.0, base=0, channel_multiplier=-1), r=[U], w=[U])
    op("pool", lambda e: e.memset(MLS[:], 1.0), w=[MLS])
    op("pool", lambda e: e.affine_select(out=MLS[:], in_=MLS[:], pattern=[[-1, 128]], compare_op=ALU.is_gt,
                                         fill=0.0, base=0, channel_multiplier=1), r=[MLS], w=[MLS])
    op("pool", lambda e: e.iota(ramp[:], pattern=[[1, TT]], base=1, channel_multiplier=0,
                                allow_small_or_imprecise_dtypes=True), w=[ramp])
    for t_ in (SST, CCA, CCB, CLRU, CS5, BG):
        op("pool", lambda e, t_=t_: e.memset(t_[:], 0.0), w=[t_])
    for w_ in WK:
        op("pool", lambda e, w_=w_: e.memset(w_[:], 0.0), w=[w_])

    A(SC[:], CT[:], AF.Silu, r=[CT], w=[SC])
    mwi = 0
    for l in range(4):
        for t in range(24):
            mw = MW[mwi % 2]; mwi += 1
            dma(mw[:], d_modw[l, :, :, t * 128:(t + 1) * 128], w=[mw])
            pb = nextps()
            for dc in range(8):
                mm(pb[:, 0:17], mw[:, dc, :], SC[:, dc, :], dc == 0, dc == 7, r=[mw, SC], w=[pb])
            A(MODT[:, l, t, :], pb[:, 0:17], AF.Identity, r=[pb, SP], w=[(MODT, l)], bias=spcol("mod_b", l * 24 + t))

    g0, g1, g2, g3 = GT[0], GT[1], GT[2], GT[3]

    def ln1p_series(dT, eT, n):
        za, wa = GT[4], GT[5]
        d, e_ap = dT[:, 0:n], eT[:, 0:n]
        ts("dve", za[:, 0:n], e_ap, 2.0, ADD, r=[eT], w=[za])
        op("dve", lambda e: e.reciprocal(out=za[:, 0:n], in_=za[:, 0:n]), r=[za], w=[za])
        tt("dve", za[:, 0:n], za[:, 0:n], e_ap, MUL, r=[za, eT], w=[za])
        tt("dve", wa[:, 0:n], za[:, 0:n], za[:, 0:n], MUL, r=[za], w=[wa])
        ts("dve", d, wa[:, 0:n], 1.0 / 13.0, MUL, r=[wa], w=[dT])
        for cst in (1.0 / 11, 1.0 / 9, 1.0 / 7, 1.0 / 5, 1.0 / 3):
            stt(d, d, cst, wa[:, 0:n], ADD, MUL, r=[dT, wa], w=[dT])
        stt(d, d, 1.0, za[:, 0:n], ADD, MUL, r=[dT, za], w=[dT])
        ts("dve", d, d, 2.0, MUL, r=[dT], w=[dT])

    A(g0[:, 0:16], spc("a_param"), AF.Exp, r=[SP], w=[g0], scale=-1.0)
    ln1p_series(g1, g0, 16)
    ts("dve", C1[:].rearrange("p a b -> p (a b)"), g1[:, 0:16], -8.0, MUL, r=[g1], w=[C1])
    ts("dve", C2[:].rearrange("p a b -> p (a b)"), g1[:, 0:16], -16.0, MUL, r=[g1], w=[C2])

    def s5p(i):
        return S5P[:, i, :]
    are, aim, ldt = spc("s5_are"), spc("s5_aim"), spc("s5_ldt")
    A(s5p(0), ldt, AF.Exp, r=[SP], w=[S5P])
    tt("dve", s5p(9), are, s5p(0), MUL, r=[SP, S5P], w=[S5P])
    A(s5p(1), s5p(9), AF.Exp, r=[S5P], w=[S5P])
    tt("dve", s5p(9), aim, s5p(0), MUL, r=[SP, S5P], w=[S5P])
    ts("dve", s5p(10), s5p(9), 1.0 / TWO_PI, MUL, r=[S5P], w=[S5P])
    ts("dve", KI[:, 0:64], s5p(10), 1.0, MUL, r=[S5P], w=[KI])
    cp("dve", s5p(11), KI[:, 0:64], r=[KI], w=[S5P])
    tt("dve", s5p(8), s5p(10), s5p(11), SUB, r=[S5P], w=[S5P])
    A(s5p(3), s5p(8), AF.Sin, r=[S5P], w=[S5P], scale=TWO_PI)
    A(s5p(9), s5p(8), AF.Sin, r=[S5P], w=[S5P], scale=TWO_PI / 2)
    tt("dve", s5p(9), s5p(9), s5p(9), MUL, r=[S5P], w=[S5P])
    ts("dve", s5p(2), s5p(9), -2.0, MUL, r=[S5P], w=[S5P], s2=1.0, o1=ADD)
    tt("dve", s5p(4), s5p(1), s5p(2), MUL, r=[S5P], w=[S5P])
    tt("dve", s5p(5), s5p(1), s5p(3), MUL, r=[S5P], w=[S5P])
    tt("dve", s5p(9), are, are, MUL, r=[SP, S5P], w=[S5P])
    tt("dve", s5p(10), aim, aim, MUL, r=[SP, S5P], w=[S5P])
    tt("dve", s5p(9), s5p(9), s5p(10), ADD, r=[S5P], w=[S5P])
    op("dve", lambda e: e.reciprocal(out=s5p(9), in_=s5p(9)), r=[S5P], w=[S5P])
    ts("dve", s5p(10), s5p(4), -1.0, ADD, r=[S5P], w=[S5P])
    tt("dve", s5p(6), s5p(10), are, MUL, r=[S5P, SP], w=[S5P])
    tt("dve", s5p(11), s5p(5), aim, MUL, r=[S5P, SP], w=[S5P])
    tt("dve", s5p(6), s5p(6), s5p(11), ADD, r=[S5P], w=[S5P])
    tt("dve", s5p(6), s5p(6), s5p(9), MUL, r=[S5P], w=[S5P])
    tt("dve", s5p(7), s5p(5), are, MUL, r=[S5P, SP], w=[S5P])
    tt("dve", s5p(11), s5p(10), aim, MUL, r=[S5P, SP], w=[S5P])
    tt("dve", s5p(7), s5p(7), s5p(11), SUB, r=[S5P], w=[S5P])
    tt("dve", s5p(7), s5p(7), s5p(9), MUL, r=[S5P], w=[S5P])
    ts("dve", s5p(10), s5p(7), -1.0, MUL, r=[S5P], w=[S5P])

    def s5c(i, j, gp):
        return S5P[:, i, j * 32 + gp:j * 32 + gp + 1]

    PBLK = [(b * 512, 512) for b in range(TT // 512)]
    TWO_PI_S = 6.283185
    owi = [0]; wi = [0]

    def scan(out, d0, d1, init, r, w):
        op("dve", lambda e: e.tensor_tensor_scan(out=out, data0=d0, data1=d1, initial=init, op0=MUL, op1=ADD), r=r, w=w)

    def recip(ap, t):
        op("dve", lambda e: e.reciprocal(out=ap, in_=ap), r=[t], w=[t])

    def rms_stats(ncols, blocks):
        pbs = [nextps() for _ in blocks]
        for c in range(8):
            sq = WK[c % 2]
            A(sq[:, 0:ncols], X[:, c, 0:ncols], AF.Square, r=[(X, c)], w=[sq])
            for pb, (bo, bn) in zip(pbs, blocks):
                mm(pb[:, 0:bn], ones[:], sq[:, bo:bo + bn], c == 0, c == 7, r=[ones, sq], w=[pb])
        for pb, (bo, bn) in zip(pbs, blocks):
            A(RS[:, bo:bo + bn], pb[:, 0:bn], AF.Sqrt, r=[pb], w=[RS], bias=EPS, scale=1.0 / 1024)
        recip(RS[:, 0:ncols], RS)

    def layer_start(l, first, ncols, blocks):
        for c in range(8):
            stt(A1[:, c, :], MODT[:, l, 8 + c, :], 1.0, spcol("norm_w", l * 8 + c).to_broadcast([128, 17]), ADD, MUL,
                r=[(MODT, l), SP], w=[A1])
        rms_stats(ncols, blocks)
        for c in range(8):
            tmp = WK[2 + c % 2]
            tt("pool", tmp[:, 0:ncols], X[:, c, 0:ncols], RS[:, 0:ncols], MUL, r=[(X, c), RS], w=[tmp])
            A(H[:, c, 0:TT], tmp[:, 0:TT], AF.Identity, r=[tmp, A1, (MODT, l)], w=[(H, c)],
              scale=A1[:, c, 0:1], bias=MODT[:, l, c, 0:1])
            if first:
                tt("dve", tmp[:, TT:TW], tmp[:, TT:TW], A1[:, c, 1:17], MUL, r=[tmp, A1], w=[tmp])
                tt("dve", H[:, c, TT:TW], tmp[:, TT:TW], MODT[:, l, c, 1:17], ADD, r=[tmp, (MODT, l)], w=[(H, c)])

    PRE = {}

    def prefetch(dsrc, key):
        wt = Wt[wi[0] % 4]; wi[0] += 1
        dmac(wt[:], dsrc, wt, lambda m: m[:])
        PRE[key] = wt

    def proj(dsrc, blocks, rhsT=None, ncolw=128, key=None):
        if key is not None and key in PRE:
            wt = PRE.pop(key)
        else:
            wt = Wt[wi[0] % 4]; wi[0] += 1
            dmac(wt[:, :, 0:ncolw], dsrc, wt, lambda m: m[:, :, 0:ncolw])
        outs = []
        for (bo, bn) in blocks:
            pb = nextps()
            for dc in range(8):
                if rhsT is None:
                    mm(pb[:, 0:bn], wt[:, dc, :], H[:, dc, bo:bo + bn], dc == 0, dc == 7, r=[wt, (H, dc)], w=[pb])
                else:
                    mm(pb[:, 0:bn], wt[:, dc, :], rhsT[:, dc, bo:bo + bn], dc == 0, dc == 7, r=[wt, (rhsT, dc)], w=[pb])
            outs.append((pb, bo, bn))
        return outs

    def out_group(dW, fbase, l, blocks, ybase):
        for m in range(8):
            ow = OW[owi[0] % 2]; owi[0] += 1
            dmac(ow[:], dW[:, fbase:fbase + 4, m * 128:(m + 1) * 128], ow, lambda t_: t_[:, 0:4, :])
            for (bo, bn) in blocks:
                pb = nextps()
                for f in range(4):
                    mm(pb[:, 0:bn], ow[:, f, :], YB[:, ybase + f, bo:bo + bn], f == 0, f == 3,
                       r=[ow, (YB, ybase + f)], w=[pb])
                if bo < TT:
                    stt(X[:, m, bo:bo + bn], pb[:, 0:bn], MODT[:, l, 16 + m, 0:1], X[:, m, bo:bo + bn], MUL, ADD,
                        r=[pb, (MODT, l), (X, m)], w=[(X, m)])
                else:
                    tt("dve", RS[:, 0:16], pb[:, 0:16], MODT[:, l, 16 + m, 1:17], MUL, r=[pb, (MODT, l)], w=[RS])
                    tt("pool", X[:, m, TT:TW], X[:, m, TT:TW], RS[:, 0:16], ADD, r=[RS, (X, m)], w=[(X, m)])

    smi = [0]
    stg = [0]

    def dmac(dst_ap, src, dstT, sel):
        mw = MW[stg[0] % 2]; stg[0] += 1
        dma(sel(mw), src, w=[mw], q="act")
        cp("act", dst_ap, sel(mw), r=[mw], w=[dstT])

    def conv_unit(pouts, raw, dst, wname, widx, bias_ap, carry_ap, carry_T, d_state, o_p, o_s, first, last):
        cp("pool", raw[:, 0:3], carry_ap, r=[carry_T], w=[raw])
        for (pb, bo, bn) in pouts:
            cp("act", raw[:, 3 + bo:3 + bo + bn], pb[:, 0:bn], r=[pb], w=[raw])

        def wcol(k):
            return spcol(wname, widx * 4 + k)
        bz = 0.0 if bias_ap is None else bias_ap
        A(dst[:, 0:TT], raw[:, 3:3 + TT], AF.Identity, r=[raw, SP], w=[dst], scale=wcol(3), bias=bz)
        for k in range(3):
            stt(dst[:, 0:TT], raw[:, k:k + TT], wcol(k), dst[:, 0:TT], MUL, ADD, r=[raw, SP, dst], w=[dst])
        cp("pool", carry_ap, raw[:, TT:TT + 3], r=[raw], w=[carry_T])
        if last:
            dma(o_p, raw[:, TT:TT + 3], r=[raw], is_out=True)
        if first:
            st = SMALL[smi[0] % 4]; ns = SMALL[(smi[0] + 1) % 4]; smi[0] += 2
            dma(st[:], d_state, w=[st])
            rawS = raw[:, 3 + TT:3 + TW]
            A(dst[:, TT:TW], rawS, AF.Identity, r=[raw, SP], w=[dst], scale=wcol(3), bias=bz)
            for k in range(3):
                stt(dst[:, TT:TW], st[:, :, k], wcol(k), dst[:, TT:TW], MUL, ADD, r=[st, SP, dst], w=[dst])
            cp("pool", ns[:, :, 0:2], st[:, :, 1:3], r=[st], w=[ns])
            cp("pool", ns[:, :, 2], rawS, r=[raw, ns], w=[ns])
            dma(o_s, ns[:], r=[ns], is_out=True)

    def mark(name):
        S.marks.append((name, dict(S.cnt)))

    def ab_layer(j, seg):
        mark("ab%d_s%d_start" % (j, seg))
        l = 2 * j
        first, last = seg == 0, seg == NSEG - 1
        ncols = TW if first else TT
        blocks = PBLK + ([(TT, 16)] if first else [])
        layer_start(l, first, ncols, blocks)
        dab = d_abin[j]
        import os
        _kab = int(os.environ.get("KAB", "99"))
        if _kab <= 1:
            return
        op("pool", lambda e: e.memset(YB[:, 11, TW - 1:TW], 0.0), w=[YB])
        ylr = [View(YB[:, 4 + 2 * i_:6 + 2 * i_, :].bitcast(F32).rearrange("p a b -> p (a b)"), (YB, ("lru", i_)))
               for i_ in range(2)]
        LSETS = [(WK[0], WK[1], WK[2], WK[3], WK[4]), (WK[5], WK[6], WK[7], ylr[0], ylr[1])]

        def lru_unit(c, W):
            raw, xc, tB, tC, tD = W
            pxa = proj(dab[:, :, c * 128:(c + 1) * 128], blocks, key=("w", c * 128))
            prefetch(dab[:, :, 1024 + c * 128:1024 + (c + 1) * 128], ("w", 1024 + c * 128))
            conv_unit(pxa, raw, xc, "conv_a_w", j * 8 + c, spcol("conv_a_b", j * 8 + c), CCA[:, j, c, :], CCA,
                      d_sca[:, j, c], o_ca[:, j, c, 0, :], o_ca[:, j, c, 1:17, :], first, last)
            yield
            gw = GW[c % 2]
            dma(gw[:, 0, :], d_gxw[j, :, c, :], w=[gw]); dma(gw[:, 1, :], d_gaw[j, :, c, :], w=[gw])
            for (bo, bn) in blocks:
                pb = nextps(); mm(pb[:, 0:bn], gw[:, 0, :], xc[:, bo:bo + bn], True, True, r=[gw, xc], w=[pb])
                A(tB[:, bo:bo + bn], pb[:, 0:bn], AF.Sigmoid, r=[pb, SP], w=[tB], bias=spcol("gx_b", j * 8 + c))
                pb = nextps(); mm(pb[:, 0:bn], gw[:, 1, :], xc[:, bo:bo + bn], True, True, r=[gw, xc], w=[pb])
                A(tC[:, bo:bo + bn], pb[:, 0:bn], AF.Sigmoid, r=[pb, SP], w=[tC], bias=spcol("ga_b", j * 8 + c))
                yield
            A(tD[:, 0:ncols], tC[:, 0:ncols], AF.Exp, r=[tC, C1], w=[tD], scale=C1[:, j, c:c + 1]); yield
            A(tC[:, 0:ncols], tC[:, 0:ncols], AF.Exp, r=[tC, C2], w=[tC], scale=C2[:, j, c:c + 1]); yield
            A(tC[:, 0:ncols], tC[:, 0:ncols], AF.Sqrt, r=[tC], w=[tC], scale=-1.0, bias=1.0); yield
            if first:
                op("pool", lambda e: e.memset(tC[:, 0:1], 1.0), r=[tC], w=[tC])
            tt("pool", tB[:, 0:ncols], tB[:, 0:ncols], tC[:, 0:ncols], MUL, r=[tB, tC], w=[tB]); yield
            tt("dve", tB[:, 0:ncols], tB[:, 0:ncols], xc[:, 0:ncols], MUL, r=[tB, xc], w=[tB]); yield
            scan(xc[:, 0:TT], tD[:, 0:TT], tB[:, 0:TT], CLRU[:, j, c:c + 1], r=[tD, tB, CLRU], w=[xc]); yield
            cp("pool", CLRU[:, j, c:c + 1], xc[:, TT - 1:TT], r=[xc], w=[CLRU])
            if last:
                dma(o_lru[:, j, c, 0:1], xc[:, TT - 1:TT], r=[xc], is_out=True, allow_slow_non_contiguous=True)
            if first:
                tt("pool", xc[:, TT:TW], tD[:, TT:TW], LR0[:, c, :], MUL, r=[tD, LR0, xc], w=[xc])
                tt("pool", xc[:, TT:TW], xc[:, TT:TW], tB[:, TT:TW], ADD, r=[xc, tB], w=[xc])
                dma(o_lru[:, j, c, 1:17], xc[:, TT:TW], r=[xc], is_out=True)
            yield
            pza = proj(dab[:, :, 1024 + c * 128:1024 + (c + 1) * 128], blocks, key=("w", 1024 + c * 128))
            if c + 2 < 8:
                prefetch(dab[:, :, (c + 2) * 128:(c + 3) * 128], ("w", (c + 2) * 128))
            for (pb, bo, bn) in pza:
                A(tD[:, bo:bo + bn], pb[:, 0:bn], AF.Silu, r=[pb], w=[tD])
            yield
            tt("dve", YB[:, c % 4, 0:ncols], xc[:, 0:ncols], tD[:, 0:ncols], MUL, r=[xc, tD], w=[(YB, c % 4)]); yield

        if first:
            dma(LR0[:], d_slru[:, j], w=[LR0])
        for c0 in range(0, 8, 2):
            run_interleaved([lru_unit(c0 + q_, LSETS[q_]) for q_ in range(2)])
            if c0 % 4 == 2:
                out_group(d_about[j], (c0 // 4) * 4, l, blocks, 0)

        if _kab <= 2:
            return
        mark("ab%d_s%d_gdn" % (j, seg))
        nt = NCH + (1 if first else 0)
        n8 = nt * 8
        wt = Wt[wi[0] % 4]; wi[0] += 1
        dmac(wt[:, :, 0:16], dab[:, :, 6144:6160], wt, lambda m: m[:, :, 0:16])
        pb = nextps()
        for t in range(NCH):
            for dc in range(8):
                mm(pb[:, t * 16:(t + 1) * 16], H[:, dc, t * 128:(t + 1) * 128], wt[:, dc, 0:16], dc == 0, dc == 7,
                   r=[wt, (H, dc)], w=[pb])
        cp("act", BG[:, 0:NCH, :].rearrange("p a b -> p (a b)"), pb[:, 0:NCH * 16], r=[pb], w=[BG])
        if first:
            for dc in range(8):
                mm(pb[0:16, NCH * 16:(NCH + 1) * 16], H[:, dc, TT:TW], wt[:, dc, 0:16], dc == 0, dc == 7,
                   r=[wt, (H, dc)], w=[pb])
            cp("act", BG[0:16, NCH, :], pb[0:16, NCH * 16:(NCH + 1) * 16], r=[pb], w=[BG])

        def v3(t_):
            return t_[:, 0:n8].rearrange("p (a b) -> p a b", b=8)
        cp("pool", v3(GT[0]), BG[:, 0:nt, 0:8], r=[BG], w=[GT[0]])
        cp("pool", v3(GT[1]), BG[:, 0:nt, 8:16], r=[BG], w=[GT[1]])
        btm2 = BTM[:, 0:nt, :].rearrange("p a b -> p (a b)")
        gtm2 = GTM[:, 0:nt, :].rearrange("p a b -> p (a b)")
        A(btm2, GT[0][:, 0:n8], AF.Sigmoid, r=[GT[0]], w=[BTM])
        tt("dve", GT[1][:, 0:n8], GT[1][:, 0:n8], spc("dt_bias", j * 72, j * 72 + n8), ADD, r=[GT[1], SP], w=[GT[1]])
        A(GT[0][:, 0:n8], GT[1][:, 0:n8], AF.Abs, r=[GT[1]], w=[GT[0]])
        A(GT[0][:, 0:n8], GT[0][:, 0:n8], AF.Exp, r=[GT[0]], w=[GT[0]], scale=-1.0)
        ln1p_series(GT[2], GT[0], n8)
        A(GT[1][:, 0:n8], GT[1][:, 0:n8], AF.Relu, r=[GT[1]], w=[GT[1]])
        tt("dve", GT[2][:, 0:n8], GT[2][:, 0:n8], GT[1][:, 0:n8], ADD, r=[GT[2], GT[1]], w=[GT[2]])
        A(GT[3][:, 0:n8], spc("a_log", j * 72, j * 72 + n8), AF.Exp, r=[SP], w=[GT[3]])
        tt("dve", GT[3][:, 0:n8], GT[3][:, 0:n8], GT[2][:, 0:n8], MUL, r=[GT[3], GT[2]], w=[GT[3]])
        ts("dve", gtm2, GT[3][:, 0:n8], -1.0, MUL, r=[GT[3]], w=[GTM])
        pb = nextps()
        for t in range(NCH):
            mm(pb[:, t * 16:t * 16 + 8], U[:], GTM[:, t, :], True, True, r=[U, GTM], w=[pb])
            mm(pb[:, t * 16 + 8:t * 16 + 16], ones[:], GTM[:, t, :], True, True, r=[ones, GTM], w=[pb])
        cp("act", GCS[:].rearrange("p a b -> p (a b)"), pb[:, 0:NCH * 16], r=[pb], w=[GCS])
        A(EGC[:], GCS[:, :, 0:8], AF.Exp, r=[GCS], w=[EGC])
        tt("dve", BEGE[:], EGC[:], BTM[:, 0:NCH, :], MUL, r=[EGC, BTM], w=[BEGE])
        tt("dve", EDEC[:], GCS[:, :, 8:16], GCS[:, :, 0:8], SUB, r=[GCS], w=[EDEC])
        A(EDEC[:], EDEC[:], AF.Exp, r=[EDEC], w=[EDEC])
        A(EGL[:], GCS[:, :, 8:16], AF.Exp, r=[GCS], w=[EGL])
        if first:
            A(GT[4][0:16, 0:8], GTM[0:16, NCH, :], AF.Exp, r=[GTM], w=[GT[4]])
            for hh in range(8):
                ts("pool", RM[0:16, 0, hh, :], ident[0:16, 0:16], GT[4][0:16, hh:hh + 1], MUL, r=[ident, GT[4]], w=[RM])
                ts("pool", RM[0:16, 1, hh, :], ident[0:16, 0:16], BTM[0:16, NCH, hh:hh + 1], MUL, r=[ident, BTM], w=[RM])
            pb = nextps()
            mm(pb[:, 0:256], ones[0:16, :], RM[0:16].rearrange("p a b c -> p (a b c)"), True, True, r=[ones, RM], w=[pb])
            cp("act", EGB[:].rearrange("p a b c -> p (a b c)"), pb[:, 0:256], r=[pb], w=[EGB])

        if _kab <= 3:
            return
        op("pool", lambda e: e.memset(YB[:, 11, TW - 1:TW], 0.0), w=[YB])
        op("pool", lambda e: e.memset(WK[7][:, TW + 2:TW + 3], 0.0), w=[WK[7]])
        _kcl = int(os.environ.get("KCL", "99"))
        _kch = int(os.environ.get("KCHUNK", "1")); _ksm = int(os.environ.get("KSAMP", "1"))
        for hh in range(int(os.environ.get("KHEADS", "8"))):
            raw, Q, K, V, Z, T1, OT = WK[0], WK[1], WK[2], WK[3], WK[4], WK[5], WK[6]

            op("pool", lambda e: e.memset(YB[:, 11, TW - 1:TW], 0.0), w=[YB])
            ysu = YB[:, 4:12, :].bitcast(F32).rearrange("p a b -> p (a b)")
            SU = [View(ysu[:, i_ * (TW + 3):(i_ + 1) * (TW + 3)], (YB, ("su", i_))) for i_ in range(3)]
            raw_k, raw_v, T1k = SU

            def colnorm_g(dst, tmp, scale_sum, post):
                tt("dve", tmp[:, 0:ncols], dst[:, 0:ncols], dst[:, 0:ncols], MUL, r=[dst], w=[tmp]); yield
                for (bo, bn) in blocks:
                    pb = nextps(); mm(pb[:, 0:bn], ones[:], tmp[:, bo:bo + bn], True, True, r=[ones, tmp], w=[pb])
                    A(tmp[:, bo:bo + bn], pb[:, 0:bn], AF.Sqrt, r=[pb], w=[tmp], bias=EPS, scale=scale_sum); yield
                recip(tmp[:, 0:ncols], tmp); yield
                if post is None:
                    tt("dve", dst[:, 0:ncols], dst[:, 0:ncols], tmp[:, 0:ncols], MUL, r=[dst, tmp], w=[dst]); yield
                else:
                    stt(dst[:, 0:ncols], dst[:, 0:ncols], post, tmp[:, 0:ncols], MUL, MUL, r=[dst, tmp], w=[dst]); yield

            def chain_qkv(col0, unit, dst, rawT, tmp, post, do_norm):
                p = proj(dab[:, :, col0:col0 + 128], blocks, key=("w", col0))
                conv_unit(p, rawT, dst, "conv_b_w", j * 24 + unit, None, CCB[:, j, unit, :], CCB,
                          d_scb[:, j, unit], o_cb[:, j, unit, 0, :], o_cb[:, j, unit, 1:17, :], first, last)
                yield
                A(dst[:, 0:ncols], dst[:, 0:ncols], AF.Silu, r=[dst], w=[dst]); yield
                if do_norm:
                    yield from colnorm_g(dst, tmp, 1.0, post)

            def chain_z():
                pz = proj(dab[:, :, 5120 + hh * 128:5120 + (hh + 1) * 128], blocks, key=("w", 5120 + hh * 128))
                for (pb, bo, bn) in pz:
                    A(Z[:, bo:bo + bn], pb[:, 0:bn], AF.Silu, r=[pb], w=[Z])
                yield

            def colnorm(dst, scale_sum, post):
                for _ in colnorm_g(dst, T1, scale_sum, post):
                    pass

            run_interleaved([chain_qkv(2048 + hh * 128, hh, Q, raw, T1, 128.0 ** -0.5, True),
                             chain_qkv(3072 + hh * 128, 8 + hh, K, raw_k, T1k, None, True),
                             chain_qkv(4096 + hh * 128, 16 + hh, V, raw_v, None, None, False),
                             chain_z()])
            if hh + 1 < 8:
                prefetch(dab[:, :, 2048 + (hh + 1) * 128:2048 + (hh + 2) * 128], ("w", 2048 + (hh + 1) * 128))
                prefetch(dab[:, :, 3072 + (hh + 1) * 128:3072 + (hh + 2) * 128], ("w", 3072 + (hh + 1) * 128))
            op("pool", lambda e: e.memset(YB[:, 11, TW - 1:TW], 0.0), w=[YB])

            SK = (SST, (j, hh))

            def chunk_par(t, c_, psn):
                sl = slice(t * 128, (t + 1) * 128)
                bcol = BTM[:, t, hh:hh + 1]
                pa = psn()
                op("pe", lambda e: e.transpose(pa[:, 0:128], K[:, sl], ident[:]), r=[K, ident], w=[pa]); yield
                A(c_["kbg"][:], pa[:, 0:128], AF.Identity, r=[pa, BEGE], w=[c_["kbg"]], scale=BEGE[:, t, hh:hh + 1]); yield
                A(c_["kdec"][:], pa[:, 0:128], AF.Identity, r=[pa, EDEC], w=[c_["kdec"]], scale=EDEC[:, t, hh:hh + 1]); yield
                pv = psn()
                op("pe", lambda e: e.transpose(pv[:, 0:128], V[:, sl], ident[:]), r=[V, ident], w=[pv]); yield
                A(c_["vb"][:], pv[:, 0:128], AF.Identity, r=[pv, BTM], w=[c_["vb"]], scale=bcol); yield
                ts("pool", c_["gbc"][:], ones[:], GTM[:, t, hh:hh + 1], MUL, r=[ones, GTM], w=[c_["gbc"]]); yield
                pe_ = psn(); mm(pe_[:, 0:128], c_["gbc"][:], U[:], True, True, r=[c_["gbc"], U], w=[pe_]); yield
                gcc = GCS[:, t, hh:hh + 1]
                cp("act", c_["gcr"][:], pe_[:, 0:128], r=[pe_], w=[c_["gcr"]]); yield
                stt(c_["E1"][:], c_["gcr"][:], gcc, zeros[:], SUB, ALU.max, r=[c_["gcr"], GCS, zeros], w=[c_["E1"]]); yield
                stt(c_["E1T"][:], c_["gcr"][:], gcc, zeros[:], SUB, ALU.min, r=[c_["gcr"], GCS, zeros], w=[c_["E1T"]]); yield
                A(c_["decL"][:], c_["E1"][:], AF.Exp, r=[c_["E1"]], w=[c_["decL"]], scale=-1.0); yield
                tt("pool", c_["decL"][:], c_["decL"][:], MLS[:], MUL, r=[c_["decL"], MLS], w=[c_["decL"]]); yield
                pc = psn(); mm(pc[:, 0:128], K[:, sl], K[:, sl], True, True, r=[K], w=[pc]); yield
                stt(c_["A0"][:], pc[:, 0:128], bcol, c_["decL"][:], MUL, MUL, r=[pc, BTM, c_["decL"]], w=[c_["A0"]]); yield
                A(c_["decU"][:], c_["E1T"][:], AF.Exp, r=[c_["E1T"]], w=[c_["decU"]]); yield
                tt("pool", c_["decU"][:], c_["decU"][:], U[:], MUL, r=[c_["decU"], U], w=[c_["decU"]]); yield
                pd = psn(); mm(pd[:, 0:128], K[:, sl], Q[:, sl], True, True, r=[K, Q], w=[pd]); yield
                tt("dve", c_["QKmT"][:], pd[:, 0:128], c_["decU"][:], MUL, r=[pd, c_["decU"]], w=[c_["QKmT"]]); yield
                A(c_["egr"][:], c_["gcr"][:], AF.Exp, r=[c_["gcr"]], w=[c_["egr"]]); yield
                tt("pool", c_["qg"][:], Q[:, sl], c_["egr"][:], MUL, r=[Q, c_["egr"]], w=[c_["qg"]]); yield
                pf = psn()
                op("pe", lambda e: e.transpose(pf[:, 0:128], c_["A0"][:], ident[:]), r=[c_["A0"], ident], w=[pf]); yield
                cp("act", c_["B0"][:], pf[:, 0:128], r=[pf], w=[c_["B0"]]); yield
                A16, B16 = c_["Aa"], c_["Ba"]
                tt("pool", A16[:], c_["A0"][:], MK[:, 0, :], MUL, r=[c_["A0"], MK], w=[A16]); yield
                tt("pool", B16[:], c_["B0"][:], MK[:, 0, :], MUL, r=[c_["B0"], MK], w=[B16]); yield
                Pp, Rp = c_["P0"], c_["R0"]
                tt("dve", Pp[:], ident[:], B16[:], SUB, r=[B16, ident], w=[Pp]); yield
                tt("dve", Rp[:], ident[:], A16[:], SUB, r=[A16, ident], w=[Rp]); yield
                Ap, Bp = A16, B16
                Abuf, Bbuf, Pbuf, Rbuf = [c_["Ab"], c_["Aa"]], [c_["Bb"], c_["Ba"]], [c_["P1"], c_["P0"]], [c_["R1"], c_["R0"]]
                for k in range(1, 4):
                    pg = psn(); mm(pg[:, 0:128], Bp[:], Ap[:], True, True, r=[Bp, Ap], w=[pg]); yield
                    An = Abuf[(k - 1) % 2]; cp("act", An[:], pg[:, 0:128], r=[pg], w=[An]); yield
                    ph = psn(); mm(ph[:, 0:128], Ap[:], Bp[:], True, True, r=[Bp, Ap], w=[ph]); yield
                    Bn = Bbuf[(k - 1) % 2]; cp("dve", Bn[:], ph[:, 0:128], r=[ph], w=[Bn]); yield
                    pi_ = psn(); mm(pi_[:, 0:128], An[:], Pp[:], True, True, r=[An, Pp], w=[pi_]); yield
                    Pn = Pbuf[(k - 1) % 2]
                    tt("dve", Pn[:], Pp[:], pi_[:, 0:128], ADD, r=[Pp, pi_], w=[Pn]); yield
                    pq = psn(); mm(pq[:, 0:128], Bn[:], Rp[:], True, True, r=[Bn, Rp], w=[pq]); yield
                    Rn = Rbuf[(k - 1) % 2]
                    tt("dve", Rn[:], Rp[:], pq[:, 0:128], ADD, r=[Rp, pq], w=[Rn]); yield
                    Ap, Bp, Pp, Rp = An, Bn, Pn, Rn
                for m in range(1, 4):
                    AM, BM, Xa, Xb = c_["Aa"], c_["Ba"], c_["Ab"], c_["Bb"]
                    tt("pool", AM[:], c_["A0"][:], MK[:, m, :], MUL, r=[c_["A0"], MK], w=[AM]); yield
                    px = psn(); mm(px[:, 0:128], AM[:], Pp[:], True, True, r=[AM, Pp], w=[px]); yield
                    cp("act", Xa[:], px[:, 0:128], r=[px], w=[Xa]); yield
                    py = psn(); mm(py[:, 0:128], Rp[:], Xa[:], True, True, r=[Rp, Xa], w=[py]); yield
                    Pn = c_["P0"] if Pp is c_["P1"] else c_["P1"]
                    tt("dve", Pn[:], Pp[:], py[:, 0:128], SUB, r=[Pp, py], w=[Pn]); yield
                    if m < 3:
                        tt("pool", BM[:], c_["B0"][:], MK[:, m, :], MUL, r=[c_["B0"], MK], w=[BM]); yield
                        px2 = psn(); mm(px2[:, 0:128], BM[:], Rp[:], True, True, r=[BM, Rp], w=[px2]); yield
                        cp("act", Xb[:], px2[:, 0:128], r=[px2], w=[Xb]); yield
                        py2 = psn(); mm(py2[:, 0:128], Pp[:], Xb[:], True, True, r=[Pp, Xb], w=[py2]); yield
                        Rn = c_["R0"] if Rp is c_["R1"] else c_["R1"]
                        tt("dve", Rn[:], Rp[:], py2[:, 0:128], SUB, r=[Rp, py2], w=[Rn]); yield
                        Rp = Rn
                    Pp = Pn
                TTt = Pp
                pj = psn(); mm(pj[:, 0:128], TTt[:], c_["vb"][:], True, True, r=[TTt, c_["vb"]], w=[pj]); yield
                cp("act", c_["u"][:], pj[:, 0:128], r=[pj], w=[c_["u"]]); yield
                pk = psn(); mm(pk[:, 0:128], c_["kbg"][:], TTt[:], True, True, r=[TTt, c_["kbg"]], w=[pk]); yield
                cp("dve", c_["wT"][:], pk[:, 0:128], r=[pk], w=[c_["wT"]]); yield

            def chunk_seq(t, c_, psn):
                sl = slice(t * 128, (t + 1) * 128)
                if t % 2 == 0:
                    Scur, ScurR, Snxt, SnxtR = SST[:, j, hh, :], SK, STMP[:], STMP
                else:
                    Scur, ScurR, Snxt, SnxtR = STMP[:], STMP, SST[:, j, hh, :], SK
                pl = psn(); mm(pl[:, 0:128], c_["wT"][:], Scur, True, True, r=[c_["wT"], ScurR], w=[pl])
                tt("dve", c_["vnew"][:], c_["u"][:], pl[:, 0:128], SUB, r=[c_["u"], pl], w=[c_["vnew"]])
                pm = psn()
                mm(pm[:, 0:128], Scur, c_["qg"][:], True, False, r=[ScurR, c_["qg"]], w=[pm])
                mm(pm[:, 0:128], c_["vnew"][:], c_["QKmT"][:], False, True, r=[c_["vnew"], c_["QKmT"]], w=[pm])
                cp("act", OT[:, sl], pm[:, 0:128], r=[pm], w=[OT])
                pn = psn(); mm(pn[:, 0:128], c_["kdec"][:], c_["vnew"][:], True, True, r=[c_["kdec"], c_["vnew"]], w=[pn])
                stt(Snxt, Scur, EGL[:, t, hh:hh + 1], pn[:, 0:128], MUL, ADD, r=[ScurR, EGL, pn], w=[SnxtR])

            mark("L%d_s%d_h%d_chunks" % (j, seg, hh))
            for t0 in range(0, NCH if _kch else 0, NSTR):
                run_interleaved([chunk_par(t0 + q, CKS[q], PSN[q]) for q in range(NSTR)])
                for q in range(NSTR):
                    chunk_seq(t0 + q, CKS[q], PSN[q])
            if last:
                dma(o_delta[j, 0, hh], SST[:, j, hh, :], r=[SK], is_out=True)
            mark("L%d_s%d_h%d_samples" % (j, seg, hh))
            if first and _ksm:
                save = psrot["lst"]
                po = PS[7]
                psrot["lst"] = [x for x in save if x != 7]
                def sample_step(s, q):
                    s0, sn, dg = hpool[3 * q], hpool[3 * q + 1], hpool[3 * q + 2]
                    pq_ = PS[q]
                    d0, d1 = DFM[:, 2 * q:2 * q + 1], DFM[:, 2 * q + 1:2 * q + 2]
                    kcol = K[:, TT + s:TT + s + 1]
                    dma(s0[:], d_sdelta[j, s, hh], w=[s0]); yield
                    mm(pq_[:, 0:1], s0[:], kcol, True, True, r=[s0, K], w=[pq_]); yield
                    stt(d0, pq_[:, 0:1], EGB[:, 0, hh, s:s + 1], V[:, TT + s:TT + s + 1], MUL, SUB,
                        r=[pq_, EGB, V], w=[(DFM, q)]); yield
                    stt(d1, d0, EGB[:, 1, hh, s:s + 1], negones[:, 0:1], MUL, MUL,
                        r=[(DFM, q), EGB, negones], w=[(DFM, q)]); yield
                    ts("pool", dg[:], ident[:], d1, MUL, r=[ident, (DFM, q)], w=[dg]); yield
                    mm(pq_[:, 128:256], ones[:], dg[:], True, True, r=[ones, dg], w=[pq_]); yield
                    ts("dve", sn[:], s0[:], EGB[:, 0, hh, s:s + 1], MUL, r=[s0, EGB], w=[sn]); yield
                    stt(sn[:], pq_[:, 128:256], kcol, sn[:], MUL, ADD, r=[pq_, K, sn], w=[sn]); yield
                    dma(o_delta[j, 1 + s, hh], sn[:], r=[sn], is_out=True); yield
                    mm(po[:, s:s + 1], sn[:], Q[:, TT + s:TT + s + 1], True, True, r=[sn, Q], w=[po]); yield

                NSS = 6
                for s_ in range(0, 16, NSS):
                    run_interleaved([sample_step(s_ + q, q) for q in range(NSS) if s_ + q < 16])
                cp("act", OT[:, TT:TW], po[:, 0:16], r=[po], w=[OT])
                psrot["lst"] = save
            mark("L%d_s%d_h%d_onorm" % (j, seg, hh))
            colnorm(OT, 1.0 / 128, None)
            stt(YB[:, hh % 4, 0:ncols], OT[:, 0:ncols], spcol("gnorm_w", j), Z[:, 0:ncols], MUL, MUL,
                r=[OT, SP, Z], w=[(YB, hh % 4)])
            if hh % 4 == 3:
                out_group(d_about[j], 8 + (hh // 4) * 4, l, blocks, 0)
        op("pool", lambda e: e.memset(YB[:, 11, TW - 1:TW], 0.0), w=[YB])
        op("pool", lambda e: e.memset(WK[7][:, TW + 2:TW + 3], 0.0), w=[WK[7]])

    def s5_layer(j, seg):
        mark("s5%d_s%d_start" % (j, seg))
        l = 2 * j + 1
        first, last = seg == 0, seg == NSEG - 1
        ncols = TW if first else TT
        blocks = PBLK + ([(TT, 16)] if first else [])
        layer_start(l, first, ncols, blocks)
        nyb = len(PBLK)
        ybanks = [7 - i for i in range(nyb)]
        ysb = 7 - nyb
        save = psrot["lst"]
        psrot["lst"] = [x for x in save if x not in ybanks and x != ysb]
        cosT, sinT, bre, bim, tA, tB, yf, Uf = WK
        Ub = XB[2]
        for c in range(8):
            pu = proj(d_cin[j][:, :, c * 128:(c + 1) * 128], blocks)
            for (pb, bo, bn) in pu:
                cp("act", Uf[:, bo:bo + bn], pb[:, 0:bn], r=[pb], w=[Uf])
            cp("pool", Ub[:, 0:ncols], Uf[:, 0:ncols], r=[Uf], w=[Ub])
            for q in range(4):
                gp = c * 4 + q
                bp = BP[gp % 2]; cr = CR[gp % 2]; cf = CF[gp % 2]
                dmac(bp[:, 0, :], d_s5B[j, 0, gp], bp, lambda m: m[:, 0, :]); dmac(bp[:, 1, :], d_s5B[j, 1, gp], bp, lambda m: m[:, 1, :])
                dma(cr[:, 0, :], d_s5C[j, 0, gp], w=[cr]); dma(cr[:, 1, :], d_s5C[j, 1, gp], w=[cr])
                cre_s, cim_s, ncim_s = s5c(6, j, gp), s5c(7, j, gp), s5c(10, j, gp)
                ts("dve", CTMP[:], cr[:, 1, :], cim_s, MUL, r=[cr, S5P], w=[CTMP])
                stt(cf[:, 0, :], cr[:, 0, :], cre_s, CTMP[:], MUL, SUB, r=[cr, S5P, CTMP], w=[cf])
                ts("dve", CTMP[:], cr[:, 1, :], cre_s, MUL, r=[cr, S5P], w=[CTMP])
                stt(cf[:, 1, :], cr[:, 0, :], ncim_s, CTMP[:], MUL, SUB, r=[cr, S5P, CTMP], w=[cf])
                SB = 512
                NSB = TT // SB

                def tables(g2_, h):
                    th = s5c(8, j, g2_)
                    C_, S_, K_ = cosT[:, h * SB:(h + 1) * SB], sinT[:, h * SB:(h + 1) * SB], KI[:, h * SB:(h + 1) * SB]
                    rc, rs_, rk = (cosT, h), (sinT, h), (KI, h)
                    ts("dve", K_, ramp[:, 0:SB], th, MUL, r=[ramp, S5P], w=[rk]); yield
                    cp("dve", S_, K_, r=[rk], w=[rs_]); yield
                    stt(C_, ramp[:, 0:SB], th, S_, MUL, SUB, r=[ramp, S5P, rs_], w=[rc]); yield
                    A(S_, C_, AF.Sin, r=[rc], w=[rs_], scale=TWO_PI_S); yield
                    A(C_, C_, AF.Sin, r=[rc], w=[rc], scale=TWO_PI_S / 2); yield
                    A(C_, C_, AF.Square, r=[rc], w=[rc]); yield
                    A(C_, C_, AF.Identity, r=[rc], w=[rc], scale=-2.0, bias=1.0); yield

                def chain(sb_, h):
                    sl = slice(sb_ * SB, (sb_ + 1) * SB)
                    c_, s_ = cosT[:, h * SB:(h + 1) * SB], sinT[:, h * SB:(h + 1) * SB]
                    rc, rs_ = (cosT, h), (sinT, h)
                    br, bi_, a_, b_ = bre[:, sl], bim[:, sl], tA[:, sl], tB[:, sl]
                    kb, ki_, ka, kb2 = (bre, sb_), (bim, sb_), (tA, sb_), (tB, sb_)
                    x0, x1 = (XB[0], sb_), (XB[1], sb_)
                    p1 = nextps(); mm(p1[:, 0:SB], bp[:, 0, :], Ub[:, sl], True, True, r=[bp, Ub], w=[p1]); yield
                    cp("act", br, p1[:, 0:SB], r=[p1], w=[kb]); yield
                    p2 = nextps(); mm(p2[:, 0:SB], bp[:, 1, :], Ub[:, sl], True, True, r=[bp, Ub], w=[p2]); yield
                    cp("act", bi_, p2[:, 0:SB], r=[p2], w=[ki_]); yield
                    tt("pool", a_, c_, br, MUL, r=[rc, kb], w=[ka]); yield
                    tt("dve", b_, s_, bi_, MUL, r=[rs_, ki_], w=[kb2]); yield
                    tt("dve", a_, a_, b_, ADD, r=[ka, kb2], w=[ka]); yield
                    tt("pool", b_, c_, bi_, MUL, r=[rc, ki_], w=[kb2]); yield
                    tt("dve", br, s_, br, MUL, r=[rs_, kb], w=[kb]); yield
                    tt("dve", b_, b_, br, SUB, r=[kb2, kb], w=[kb2]); yield
                    rrbc = s5c(1, j, gp).to_broadcast([128, SB])
                    if sb_ == 0:
                        ire, iim, rin = CS5[:, j, 0, gp:gp + 1], CS5[:, j, 1, gp:gp + 1], (CS5, (j, gp))
                    else:
                        ire, iim, rin = CMID[:, sb_ - 1, 0:1], CMID[:, sb_ - 1, 1:2], (CMID, sb_ - 1)
                    if sb_ == NSB - 1:
                        ore, oim, rout = CS5[:, j, 0, gp:gp + 1], CS5[:, j, 1, gp:gp + 1], (CS5, (j, gp))
                    else:
                        ore, oim, rout = CMID[:, sb_, 0:1], CMID[:, sb_, 1:2], (CMID, sb_)
                    scan(bi_, rrbc, a_, ire, r=[S5P, ka, rin], w=[ki_]); yield
                    scan(br, rrbc, b_, iim, r=[S5P, kb2, rin], w=[kb]); yield
                    tt("pool", a_, c_, bi_, MUL, r=[rc, ki_], w=[ka]); yield
                    tt("dve", b_, s_, br, MUL, r=[rs_, kb], w=[kb2]); yield
                    tt("dve", XB[0][:, sl], a_, b_, SUB, r=[ka, kb2], w=[x0]); yield
                    tt("dve", ore, tA[:, (sb_ + 1) * SB - 1:(sb_ + 1) * SB], tB[:, (sb_ + 1) * SB - 1:(sb_ + 1) * SB], SUB,
                       r=[ka, kb2], w=[rout]); yield
                    tt("pool", a_, s_, bi_, MUL, r=[rs_, ki_], w=[ka]); yield
                    tt("dve", b_, c_, br, MUL, r=[rc, kb], w=[kb2]); yield
                    tt("dve", XB[1][:, sl], a_, b_, ADD, r=[ka, kb2], w=[x1]); yield
                    tt("dve", oim, tA[:, (sb_ + 1) * SB - 1:(sb_ + 1) * SB], tB[:, (sb_ + 1) * SB - 1:(sb_ + 1) * SB], ADD,
                       r=[ka, kb2], w=[rout]); yield
                    yb = PS[ybanks[sb_]]
                    mm(yb[:, 0:SB], cf[:, 0, :], XB[0][:, sl], q == 0, False, r=[cf, x0], w=[yb]); yield
                    mm(yb[:, 0:SB], cf[:, 1, :], XB[1][:, sl], False, q == 3, r=[cf, x1], w=[yb]); yield

                hbuf = gp % 2
                if gp == 0:
                    run_interleaved([tables(0, 0)])
                def delayed(g_, n_):
                    for _ in range(n_):
                        yield
                    yield from g_
                gens = [delayed(chain(sb_, hbuf), 9 * sb_) for sb_ in range(NSB)]
                if gp + 1 < 32:
                    gens.append(tables(gp + 1, 1 - hbuf))
                run_interleaved(gens)
                if first:
                    dma(S5S[:, 0, :], d_ss5re[:, j, gp, :], w=[S5S]); dma(S5S[:, 1, :], d_ss5im[:, j, gp, :], w=[S5S])
                    p1 = nextps()
                    mm(p1[:, 0:16], bp[:, 0, :], Ub[:, TT:TW], True, True, r=[bp, Ub], w=[p1])
                    mm(p1[:, 16:32], bp[:, 1, :], Ub[:, TT:TW], True, True, r=[bp, Ub], w=[p1])
                    cp("act", S5T[:].rearrange("p a b -> p (a b)"), p1[:, 0:32], r=[p1], w=[S5T])
                    Are, Aim = s5c(4, j, gp), s5c(5, j, gp)
                    x0r, x0i, pr_, pi2 = S5S[:, 0, :], S5S[:, 1, :], S5T[:, 0, :], S5T[:, 1, :]
                    N0, N1, N2 = S5N[:, 0, :], S5N[:, 1, :], S5N[:, 2, :]
                    ts("dve", S5U[:], x0i, Aim, MUL, r=[S5S, S5P], w=[S5U])
                    stt(N0, x0r, Are, S5U[:], MUL, SUB, r=[S5S, S5P, S5U], w=[S5N])
                    stt(N0, pr_, cre_s, N0, MUL, ADD, r=[S5T, S5P, S5N], w=[S5N])
                    stt(N0, pi2, ncim_s, N0, MUL, ADD, r=[S5T, S5P, S5N], w=[S5N])
                    ts("dve", S5U[:], x0r, Aim, MUL, r=[S5S, S5P], w=[S5U])
                    stt(N1, x0i, Are, S5U[:], MUL, ADD, r=[S5S, S5P, S5U], w=[S5N])
                    stt(N1, pi2, cre_s, N1, MUL, ADD, r=[S5T, S5P, S5N], w=[S5N])
                    stt(N1, pr_, cim_s, N1, MUL, ADD, r=[S5T, S5P, S5N], w=[S5N])
                    ts("dve", N2, N1, -1.0, MUL, r=[S5N], w=[S5N])
                    dma(o_s5re[:, j, gp, :], N0, r=[S5N], is_out=True)
                    dma(o_s5im[:, j, gp, :], N1, r=[S5N], is_out=True)
                    ys = PS[ysb]
                    mm(ys[:, 0:16], cr[:, 0, :], N0, q == 0, False, r=[cr, S5N], w=[ys])
                    mm(ys[:, 0:16], cr[:, 1, :], N2, False, q == 3, r=[cr, S5N], w=[ys])
            dcol = spcol("s5_d", j * 8 + c)
            for bi2, (bo, bn) in enumerate(PBLK):
                yb = PS[ybanks[bi2]]
                stt(yf[:, bo:bo + bn], Uf[:, bo:bo + bn], dcol, yb[:, 0:bn], MUL, ADD, r=[Uf, SP, yb], w=[yf])
            if first:
                stt(yf[:, TT:TW], Uf[:, TT:TW], dcol, PS[ysb][:, 0:16], MUL, ADD, r=[Uf, SP, PS[ysb]], w=[yf])
            A(YB[:, c, 0:ncols], yf[:, 0:ncols], AF.Gelu_apprx_tanh, r=[yf], w=[(YB, c)])
        psrot["lst"] = save
        if last:
            cr_, ci_ = S5P[:, 6, j * 32:(j + 1) * 32], S5P[:, 7, j * 32:(j + 1) * 32]
            xr, xi = CS5[:, j, 0, :], CS5[:, j, 1, :]
            tt("dve", S5O[:, 0, :], ci_, xi, MUL, r=[S5P, CS5], w=[S5O])
            tt("dve", S5O[:, 1, :], cr_, xr, MUL, r=[S5P, CS5], w=[S5O])
            tt("dve", S5O[:, 0, :], S5O[:, 1, :], S5O[:, 0, :], SUB, r=[S5O], w=[S5O])
            dma(o_s5pre[:, j, :], S5O[:, 0, :], r=[S5O], is_out=True)
            tt("dve", S5O[:, 0, :], ci_, xr, MUL, r=[S5P, CS5, S5O], w=[S5O])
            tt("dve", S5O[:, 1, :], cr_, xi, MUL, r=[S5P, CS5], w=[S5O])
            tt("dve", S5O[:, 1, :], S5O[:, 1, :], S5O[:, 0, :], ADD, r=[S5O], w=[S5O])
            dma(o_s5pim[:, j, :], S5O[:, 1, :], r=[S5O], is_out=True)
        mark("s5%d_s%d_glu" % (j, seg))
        sg, zs = WK[0], WK[1]
        for m in range(8):
            pgl = proj(d_gluw[j][:, :, m * 128:(m + 1) * 128], blocks, rhsT=YB)
            for (pb, bo, bn) in pgl:
                A(sg[:, bo:bo + bn], pb[:, 0:bn], AF.Sigmoid, r=[pb, SP], w=[sg], bias=spcol("glu_b", j * 8 + m))
            pz = proj(d_cin[j][:, :, 1024 + m * 128:1024 + (m + 1) * 128], blocks)
            for (pb, bo, bn) in pz:
                A(zs[:, bo:bo + bn], pb[:, 0:bn], AF.Silu, r=[pb], w=[zs])
            tt("pool", sg[:, 0:ncols], sg[:, 0:ncols], zs[:, 0:ncols], MUL, r=[sg, zs], w=[sg])
            tt("dve", YB[:, 8 + m % 4, 0:ncols], sg[:, 0:ncols], YB[:, m, 0:ncols], MUL, r=[sg, (YB, m)], w=[(YB, 8 + m % 4)])
            if m % 4 == 3:
                out_group(d_cout[j], (m // 4) * 4, l, blocks, 8)

    for seg in range(NSEG):
        first = seg == 0
        ncols = TW if first else TT
        blocks = PBLK + ([(TT, 16)] if first else [])
        for c in range(8):
            dma(X[:, c, 0:TT], d_xp[:, c, seg * TT:(seg + 1) * TT], w=[(X, c)])
            if first:
                dma(X[:, c, TT:TW], d_xs[:, c, :], w=[(X, c)])
        import os
        _ks = int(os.environ.get("KSTOP", "99"))
        if _ks >= 1:
            ab_layer(0, seg)
        if _ks >= 2:
            s5_layer(0, seg)
        if _ks >= 3:
            ab_layer(1, seg); s5_layer(1, seg)
        mark("final_s%d" % seg)
        rms_stats(ncols, blocks)
        for c in range(8):
            tmp = WK[2 + c % 2]
            tt("pool", tmp[:, 0:ncols], X[:, c, 0:ncols], RS[:, 0:ncols], MUL, r=[(X, c), RS], w=[tmp])
            ts("dve", tmp[:, 0:ncols], tmp[:, 0:ncols], spcol("fnorm_w", c), MUL, r=[tmp, SP], w=[tmp])
            dma(o_y[:, c, seg * TT:(seg + 1) * TT], tmp[:, 0:TT], r=[tmp], is_out=True)
            if first:
                dma(o_y[:, c, 2048:2064], tmp[:, TT:TW], r=[tmp], is_out=True)
    mark("end")
    S.finish()
    return nc, S


def _fm(v, n):
    v = np.asarray(v, np.float32)
    lead = v.shape[:-1]
    return np.ascontiguousarray(np.moveaxis(v.reshape(lead + (n, 128)), -1, 0))


def _state_layout(a):
    L = a.shape[0]
    return np.ascontiguousarray(a.reshape(L, 32, 2, 64).transpose(2, 3, 0, 1).reshape(128, L, 32))


def kernel(**inp):
    f32 = np.float32
    g = {k: np.asarray(v) for k, v in inp.items()}
    lay, SPN = sp_layout()
    sp = np.zeros((128, SPN), f32)

    def put(name, arr):
        off, w = lay[name]
        a = np.asarray(arr, f32).reshape(128, -1)
        assert a.shape[1] == w, (name, a.shape, w)
        sp[:, off:off + w] = a

    put("mod_b", _fm(g["mod_b"], 24)); put("norm_w", _fm(g["norm_w"], 8)); put("fnorm_w", _fm(g["final_norm_w"], 8))
    put("conv_a_w", g["conv_a_w"].reshape(2, 4, 8, 128).transpose(3, 0, 2, 1))
    put("conv_a_b", _fm(g["conv_a_b"], 8)); put("gx_b", _fm(g["lru_gx_b"], 8)); put("ga_b", _fm(g["lru_ga_b"], 8))
    put("a_param", _fm(g["lru_a_param"], 8))
    put("conv_b_w", g["conv_b_w"].reshape(2, 4, 24, 128).transpose(3, 0, 2, 1))
    put("gnorm_w", g["gdn_norm_w"].T)
    put("a_log", np.broadcast_to(g["gdn_a_log"][None, :, None, :], (128, 2, 9, 8)))
    put("dt_bias", np.broadcast_to(g["gdn_dt_bias"][None, :, None, :], (128, 2, 9, 8)))
    put("s5_are", _state_layout(g["s5_a_re"])); put("s5_aim", _state_layout(g["s5_a_im"]))
    put("s5_ldt", _state_layout(np.broadcast_to(g["s5_log_dt"][:, :, None], (2, 64, 64))))
    put("s5_d", _fm(g["s5_d"], 8)); put("glu_b", _fm(g["glu_b"], 8))

    def wl(w, n):
        L, _, C = w.shape
        return np.ascontiguousarray(w.reshape(L, n, 128, C).transpose(0, 2, 1, 3).astype(f32))

    s5B = np.zeros((2, 2, 32, 128, 128), f32); s5C = np.zeros((2, 2, 32, 128, 128), f32)
    for ri, (b, c) in enumerate(((g["s5_b_re"], g["s5_c_re"]), (g["s5_b_im"], g["s5_c_im"]))):
        for gg in range(64):
            gp, g2, r0 = gg // 2, gg % 2, (gg % 8) * 16
            s5B[:, ri, gp, r0:r0 + 16, g2 * 64:(g2 + 1) * 64] = b[:, gg].transpose(0, 2, 1)
            s5C[:, ri, gp, g2 * 64:(g2 + 1) * 64, r0:r0 + 16] = c[:, gg].transpose(0, 2, 1)
    shared = {
        "sp": sp, "modw": wl(g["mod_w"], 8), "abin": wl(g["ab_in_w"], 8), "about": wl(g["ab_out_w"], 16),
        "gxw": np.ascontiguousarray(g["lru_gx_w"].transpose(0, 2, 1, 3).astype(f32)),
        "gaw": np.ascontiguousarray(g["lru_ga_w"].transpose(0, 2, 1, 3).astype(f32)),
        "cin": wl(g["c_in_w"], 8), "cout": wl(g["c_out_w"], 8), "gluw": wl(g["glu_w"], 8),
        "s5B": s5B, "s5C": s5C,
    }
    ii = np.arange(128)

    def bm(b):
        return (ii[:, None] // b == ii[None, :] // b).astype(f32)
    shared["mk"] = np.stack([bm(16), bm(32) - bm(16), bm(64) - bm(32), bm(128) - bm(64)], 1)
    in_maps = []
    for i in range(8):
        sl = slice(16 * i, 16 * i + 16)
        m = dict(shared)
        m["xp"] = _fm(g["x_prompt"][i], 8).transpose(0, 2, 1).copy()
        m["xs"] = _fm(g["x_sample"][sl, 0], 8).transpose(0, 2, 1).copy()
        cc = np.concatenate([g["c_prompt"][i:i + 1], g["c_sample"][sl]], 0)
        m["cT"] = _fm(cc, 8).transpose(0, 2, 1).copy()
        m["sca"] = g["state_conv_a"][:, sl].reshape(2, 16, 3, 8, 128).transpose(4, 0, 3, 1, 2).copy()
        m["slru"] = g["state_lru"][:, sl].reshape(2, 16, 8, 128).transpose(3, 0, 2, 1).copy()
        m["scb"] = g["state_conv_b"][:, sl].reshape(2, 16, 3, 24, 128).transpose(4, 0, 3, 1, 2).copy()
        m["sdelta"] = np.ascontiguousarray(g["state_delta"][:, sl])
        for nm, key in (("ss5re", "state_s5_re"), ("ss5im", "state_s5_im")):
            m[nm] = g[key][:, sl].reshape(2, 16, 32, 2, 64).transpose(3, 4, 0, 2, 1).reshape(128, 2, 32, 16).copy()
        in_maps.append({k: np.ascontiguousarray(v, dtype=f32) for k, v in m.items()})

    nc, _ = build_program()
    import os
    _ncores = int(os.environ.get("KCORES", "8"))
    res = run_bass_kernel_spmd(nc, in_maps[:_ncores], core_ids=list(range(_ncores))).results
    res = list(res) + [res[0]] * (8 - _ncores)

    y_p = np.zeros((8, 2048, 1024), f32); y_s = np.zeros((128, 1, 1024), f32)
    ca = [np.zeros((2, 8, 3, 1024), f32), np.zeros((2, 128, 3, 1024), f32)]
    lru = [np.zeros((2, 8, 1024), f32), np.zeros((2, 128, 1024), f32)]
    cb = [np.zeros((2, 8, 3, 3072), f32), np.zeros((2, 128, 3, 3072), f32)]
    dl = [np.zeros((2, 8, 8, 128, 128), f32), np.zeros((2, 128, 8, 128, 128), f32)]
    sr = [np.zeros((2, 8, 64, 64), f32), np.zeros((2, 128, 64, 64), f32)]
    si = [np.zeros((2, 8, 64, 64), f32), np.zeros((2, 128, 64, 64), f32)]
    for i in range(8):
        r = res[i]
        sl = slice(16 * i, 16 * i + 16)
        y = r["o_y"].transpose(2, 1, 0).reshape(2064, 1024)
        y_p[i] = y[:2048]; y_s[sl, 0] = y[2048:]
        a = r["o_ca"].transpose(1, 3, 4, 2, 0).reshape(2, 17, 3, 1024)
        ca[0][:, i] = a[:, 0]; ca[1][:, sl] = a[:, 1:]
        a = r["o_lru"].transpose(1, 3, 2, 0).reshape(2, 17, 1024)
        lru[0][:, i] = a[:, 0]; lru[1][:, sl] = a[:, 1:]
        a = r["o_cb"].transpose(1, 3, 4, 2, 0).reshape(2, 17, 3, 3072)
        cb[0][:, i] = a[:, 0]; cb[1][:, sl] = a[:, 1:]
        a = r["o_delta"]
        dl[0][:, i] = a[:, 0]; dl[1][:, sl] = a[:, 1:]
        for o_, nm, nmp in ((sr, "o_s5re", "o_s5pre"), (si, "o_s5im", "o_s5pim")):
            a = r[nm].reshape(2, 64, 2, 32, 16).transpose(2, 4, 3, 0, 1).reshape(2, 16, 64, 64)
            o_[1][:, sl] = a
            o_[0][:, i] = r[nmp].reshape(2, 64, 2, 32).transpose(2, 3, 0, 1).reshape(2, 64, 64)
    return (y_p, y_s, ca[0], lru[0], cb[0], dl[0], sr[0], si[0], ca[1], lru[1], cb[1], dl[1], sr[1], si[1])
```

```python
from contextlib import ExitStack
import numpy as np
import concourse.bass as bass
import concourse.mybir as mybir
from concourse.bass_utils import run_bass_kernel_spmd

F32 = mybir.dt.float32
BF16 = mybir.dt.bfloat16
AF = mybir.ActivationFunctionType
ALU = mybir.AluOpType


class _RS:
    __slots__ = ("w", "r")

    def __init__(self):
        self.w = None
        self.r = {}


class T:
    def __init__(self, h, name):
        self.h = h
        self.name = name
        self.subs = {None: _RS()}

    def __getitem__(self, idx):
        return self.h[idx]


class Sched:
    def __init__(self, nc, n_sp=12, n_pool=8, n_act=6):
        self.nc = nc
        self.es = ExitStack()
        self.eng = {"pe": nc.tensor, "dve": nc.vector, "act": nc.scalar, "pool": nc.gpsimd, "sp": nc.sync}
        self.sem = {}
        self.cnt = {}
        for e in self.eng:
            self.sem[e] = self.es.enter_context(nc.semaphore("s_" + e))
            self.cnt[e] = 0
        self.dq = {}
        for q, n in (("sp", n_sp), ("pool", n_pool), ("act", n_act)):
            sems = [self.es.enter_context(nc.semaphore("d_%s%d" % (q, i))) for i in range(n)]
            self.dq[q] = {"sems": sems, "n": 0}
        self.semobj = {}
        for e in self.eng:
            self.semobj[("e", e)] = self.sem[e]
        for q in self.dq:
            for i, s in enumerate(self.dq[q]["sems"]):
                self.semobj[("d", q, i)] = s
        self.waited = {e: {} for e in self.eng}
        self.out_tokens = []
        self.ninst = 0
        self.marks = []

    def sb(self, name, shape, dtype=F32):
        h = self.es.enter_context(self.nc.sbuf_tensor(name, list(shape), dtype))
        return T(h, name)

    def ps(self, name, shape, dtype=F32):
        h = self.es.enter_context(self.nc.psum_tensor(name, list(shape), dtype))
        return T(h, name)

    @staticmethod
    def _res(x):
        if isinstance(x, T):
            return x, None
        if hasattr(x, "res"):
            return x.res
        return x[0], x[1]

    def _collect(self, r, w):
        deps = {}

        def add(tok):
            if tok is None:
                return
            k, v = tok
            if deps.get(k, 0) < v:
                deps[k] = v

        for x in r:
            t, k = self._res(x)
            if k is None:
                for rs in t.subs.values():
                    add(rs.w)
            else:
                add(t.subs[None].w)
                if k in t.subs:
                    add(t.subs[k].w)
        for x in w:
            t, k = self._res(x)
            if k is None:
                lst = list(t.subs.values())
            else:
                lst = [t.subs[None]]
                if k in t.subs:
                    lst.append(t.subs[k])
            for rs in lst:
                add(rs.w)
                for kk, vv in rs.r.items():
                    add((kk, vv))
        return deps

    def _commit(self, r, w, tok):
        k0, v0 = tok
        for x in r:
            t, k = self._res(x)
            rs = t.subs.setdefault(k, _RS())
            if rs.r.get(k0, 0) < v0:
                rs.r[k0] = v0
        for x in w:
            t, k = self._res(x)
            if k is None:
                t.subs = {None: _RS()}
                t.subs[None].w = tok
            else:
                rs = t.subs.setdefault(k, _RS())
                rs.w = tok
                rs.r = {}

    def _waits(self, e, deps, skip_self=False):
        wd = self.waited[e]
        for k, v in deps.items():
            if skip_self and k == ("e", e):
                continue
            if wd.get(k, 0) >= v:
                continue
            self.eng[e].wait_ge(self.semobj[k], v)
            wd[k] = v

    def op(self, e, fn, r=(), w=()):
        deps = self._collect(r, w)
        self._waits(e, deps, skip_self=(e == "pe"))
        ins = fn(self.eng[e])
        self.cnt[e] += 1
        ins.then_inc(self.sem[e], 1)
        tok = (("e", e), self.cnt[e])
        self.waited[e][("e", e)] = max(self.waited[e].get(("e", e), 0), 0)
        self._commit(r, w, tok)
        self.ninst += 1
        return tok

    def dma(self, out, in_, r=(), w=(), cast=False, q=None, is_out=False, **kw):
        if q is None:
            q = "pool" if cast else "sp"
        dq = self.dq[q]
        n = dq["n"]
        dq["n"] += 1
        P = len(dq["sems"])
        i = n % P
        rnd = n // P
        deps = self._collect(r, w)
        if rnd > 0:
            k = ("d", q, i)
            if deps.get(k, 0) < 16 * rnd:
                deps[k] = 16 * rnd
        self._waits(q, deps)
        ins = self.eng[q].dma_start(out=out, in_=in_, **kw)
        ins.then_inc(dq["sems"][i], 16)
        tok = (("d", q, i), 16 * (rnd + 1))
        self._commit(r, w, tok)
        if is_out:
            self.out_tokens.append(tok)
        self.ninst += 1
        return tok

    def finish(self):
        deps = {}
        for k, v in self.out_tokens:
            if deps.get(k, 0) < v:
                deps[k] = v
        self._waits("sp", deps)
        self.es.close()


TT = 1024
NSEG = 2048 // TT
NCH = TT // 128
TW = TT + 16
EPS = 1e-6
IN_AB = 6160
TWO_PI = 6.283185307179586


def sp_layout():
    items = [("mod_b", 96), ("norm_w", 32), ("fnorm_w", 8), ("conv_a_w", 64), ("conv_a_b", 16),
             ("gx_b", 16), ("ga_b", 16), ("a_param", 16), ("conv_b_w", 192), ("gnorm_w", 2),
             ("a_log", 144), ("dt_bias", 144), ("s5_are", 64), ("s5_aim", 64), ("s5_ldt", 64),
             ("s5_d", 16), ("glu_b", 16)]
    lay = {}
    off = 0
    for n, w in items:
        lay[n] = (off, w)
        off += w
    return lay, off


def build_program():
    nc = bass.Bass("TRN2", target_bir_lowering=False)
    lay, SPN = sp_layout()

    def din(name, shape):
        return nc.dram_tensor(name, list(shape), F32, kind="ExternalInput").ap()

    def dout(name, shape):
        return nc.dram_tensor(name, list(shape), F32, kind="ExternalOutput").ap()

    d_xp = din("xp", [128, 8, 2048]); d_xs = din("xs", [128, 8, 16]); d_cT = din("cT", [128, 8, 17])
    d_sp = din("sp", [128, SPN]); d_modw = din("modw", [4, 128, 8, 3072])
    d_abin = din("abin", [2, 128, 8, IN_AB]); d_about = din("about", [2, 128, 16, 1024])
    d_gxw = din("gxw", [2, 128, 8, 128]); d_gaw = din("gaw", [2, 128, 8, 128])
    d_cin = din("cin", [2, 128, 8, 2048]); d_cout = din("cout", [2, 128, 8, 1024])
    d_gluw = din("gluw", [2, 128, 8, 1024])
    d_s5B = din("s5B", [2, 2, 32, 128, 128]); d_s5C = din("s5C", [2, 2, 32, 128, 128])
    d_sca = din("sca", [128, 2, 8, 16, 3]); d_slru = din("slru", [128, 2, 8, 16])
    d_scb = din("scb", [128, 2, 24, 16, 3]); d_sdelta = din("sdelta", [2, 16, 8, 128, 128])
    d_mk = din("mk", [128, 4, 128])
    d_ss5re = din("ss5re", [128, 2, 32, 16]); d_ss5im = din("ss5im", [128, 2, 32, 16])
    o_y = dout("o_y", [128, 8, 2064]); o_ca = dout("o_ca", [128, 2, 8, 17, 3])
    o_lru = dout("o_lru", [128, 2, 8, 17]); o_cb = dout("o_cb", [128, 2, 24, 17, 3])
    o_delta = dout("o_delta", [2, 17, 8, 128, 128])
    o_s5re = dout("o_s5re", [128, 2, 32, 16]); o_s5im = dout("o_s5im", [128, 2, 32, 16])
    o_s5pre = dout("o_s5pre", [128, 2, 32]); o_s5pim = dout("o_s5pim", [128, 2, 32])

    S = Sched(nc)
    sb, op, dma = S.sb, S.op, S.dma

    def mm(out, lhsT, rhs, start, stop, r, w):
        op("pe", lambda e: e.matmul(out, lhsT=lhsT, rhs=rhs, start=start, stop=stop), r=r, w=w)

    def A(out, in_, func, r, w, bias=None, scale=None, eng="act"):
        kw = {}
        if bias is not None:
            kw["bias"] = bias
        if scale is not None:
            kw["scale"] = scale
        op("act", lambda e: e.activation(out=out, in_=in_, func=func, **kw), r=r, w=w)

    def tt(eng, out, a, b, o, r, w):
        op(eng, lambda e: e.tensor_tensor(out=out, in0=a, in1=b, op=o), r=r, w=w)

    def ts(eng, out, a, s1, o0, r, w, s2=None, o1=None):
        if o1 is None:
            op(eng, lambda e: e.tensor_scalar(out=out, in0=a, scalar1=s1, scalar2=None, op0=o0), r=r, w=w)
        else:
            op(eng, lambda e: e.tensor_scalar(out=out, in0=a, scalar1=s1, scalar2=s2, op0=o0, op1=o1), r=r, w=w)

    def stt(out, a, s, b, o0, o1, r, w):
        op("dve", lambda e: e.scalar_tensor_tensor(out=out, in0=a, scalar=s, in1=b, op0=o0, op1=o1), r=r, w=w)

    def cp(eng, out, in_, r, w):
        if eng == "act":
            op("act", lambda e: e.copy(out=out, in_=in_), r=r, w=w)
        else:
            op(eng, lambda e: e.tensor_copy(out=out, in_=in_), r=r, w=w)

    MUL, ADD, SUB = ALU.mult, ALU.add, ALU.subtract

    X = sb("X", [128, 8, TW]); H = sb("H", [128, 8, TW], BF16)
    RS = sb("RS", [128, TW])
    SP = sb("SP", [128, SPN]); MODT = sb("MODT", [128, 4, 24, 17]); A1 = sb("A1", [128, 8, 17])
    SC = sb("SC", [128, 8, 17]); CT = sb("CT", [128, 8, 17])
    ident = sb("ident", [128, 128]); ones = sb("ones", [128, 128]); U = sb("U", [128, 128])
    MLS = sb("MLS", [128, 128]); ramp = sb("ramp", [128, TT])
    zeros = sb("zeros", [128, 128]); negones = sb("negones", [128, 1]); MK = sb("MK", [128, 4, 128])
    MW = [sb("MW%d" % i, [128, 8, 128]) for i in range(2)]
    Wt = [sb("Wt%d" % i, [128, 8, 128], BF16) for i in range(4)]
    OW = [sb("OW%d" % i, [128, 4, 128], BF16) for i in range(2)]
    GW = [sb("GW%d" % i, [128, 2, 128]) for i in range(2)]
    WK = [sb("WK%d" % i, [128, TW + 3]) for i in range(8)]
    YB = sb("YB", [128, 12, TW], BF16)
    XB = [sb("XB%d" % i, [128, TW], BF16) for i in range(3)]
    KI = sb("KI", [128, TT], mybir.dt.int32)
    SST = sb("SST", [128, 2, 8, 128]); STMP = sb("STMP", [128, 128])
    CCA = sb("CCA", [128, 2, 8, 3]); CCB = sb("CCB", [128, 2, 24, 3]); CLRU = sb("CLRU", [128, 2, 8])
    CS5 = sb("CS5", [128, 2, 2, 32]); CMID = sb("CMID", [128, 2, 2])
    C1 = sb("C1", [128, 2, 8]); C2 = sb("C2", [128, 2, 8])
    S5P = sb("S5P", [128, 12, 64])
    SMALL = [sb("SM%d" % i, [128, 16, 3]) for i in range(4)]
    LR0 = sb("LR0", [128, 8, 16])
    BG = sb("BG", [128, 9, 16]); BTM = sb("BTM", [128, 9, 8]); GTM = sb("GTM", [128, 9, 8])
    GT = [sb("GT%d" % i, [128, 72]) for i in range(6)]
    GCS = sb("GCS", [128, NCH, 16]); EGC = sb("EGC", [128, NCH, 8]); BEGE = sb("BEGE", [128, NCH, 8])
    EDEC = sb("EDEC", [128, NCH, 8]); EGL = sb("EGL", [128, NCH, 8])
    RM = sb("RM", [128, 2, 8, 16]); EGB = sb("EGB", [128, 2, 8, 16])
    CK = {n: sb("CK_" + n, [128, 128]) for n in
          ["gcr", "egr", "kbg", "kdec", "vb", "gbc", "E1", "E1T", "decL", "decU", "A0", "QKmT", "qg", "B0", "P0", "P1",
           "Aa", "Ab", "Ba", "Bb", "u", "wT", "vnew", "R0", "R1"]}
    CKX = [sb("CKX%d" % i, [128, 128]) for i in range(7)]
    S0 = []; SN = []
    DFM = sb("DFM", [128, 12])
    BP = [sb("BP%d" % i, [128, 2, 128], BF16) for i in range(2)]
    CR = [sb("CR%d" % i, [128, 2, 128]) for i in range(2)]
    CF = [sb("CF%d" % i, [128, 2, 128], BF16) for i in range(2)]
    CTMP = sb("CTMP", [128, 128])
    S5S = sb("S5S", [128, 2, 16]); S5N = sb("S5N", [128, 3, 16]); S5T = sb("S5T", [128, 2, 16]); S5U = sb("S5U", [128, 16]); S5O = sb("S5O", [128, 2, 32])
    PS = [S.ps("ps%d" % i, [128, 512]) for i in range(8)]
    psrot = {"lst": list(range(8)), "i": 0}

    class View:
        def __init__(self, ap, res):
            self.ap = ap
            self.res = res

        def __getitem__(self, idx):
            return self.ap[idx]

    def make_psn(banks):
        st = {"i": 0}

        def psn():
            p = PS[banks[st["i"] % len(banks)]]
            st["i"] += 1
            return p
        return psn

    PSN = [make_psn([2 * q_, 2 * q_ + 1]) for q_ in range(4)]

    def run_interleaved(gens):
        gens = list(gens)
        while gens:
            for g_ in list(gens):
                try:
                    next(g_)
                except StopIteration:
                    gens.remove(g_)

    ybf = YB[:, 4:12, :].bitcast(F32)
    hpool = list(CK.values())
    for i_ in range(32):
        hpool.append(View(ybf[:, i_ // 4, (i_ % 4) * 128:(i_ % 4 + 1) * 128], (YB, ("ck", i_))))
    for i_ in range(8):
        hpool.append(View(WK[7][:, i_ * 128:(i_ + 1) * 128], (WK[7], ("ck", i_))))
    hpool += CKX
    SNAMES = ["kbg", "kdec", "vb", "QKmT", "qg", "u", "wT", "vnew", "A0", "B0", "Aa", "Ab", "Ba", "Bb", "P0", "P1", "R0", "R1"]
    ALIAS = {"gbc": "P0", "gcr": "Aa", "E1": "Ab", "E1T": "Ba", "decL": "Bb", "decU": "P1", "egr": "R0"}
    NSTR = 4
    assert len(hpool) >= NSTR * len(SNAMES), len(hpool)
    CKS = []
    for q_ in range(NSTR):
        d_ = {n_: hpool[q_ * len(SNAMES) + i_] for i_, n_ in enumerate(SNAMES)}
        for a_, b_ in ALIAS.items():
            d_[a_] = d_[b_]
        CKS.append(d_)

    def nextps():
        l = psrot["lst"]
        p = PS[l[psrot["i"] % len(l)]]
        psrot["i"] += 1
        return p

    def spc(name, a=0, b=None):
        off, w = lay[name]
        if b is None:
            b = w
        return SP[:, off + a:off + b]

    def spcol(name, i):
        off, _ = lay[name]
        return SP[:, off + i:off + i + 1]

    dma(SP[:], d_sp, w=[SP]); dma(CT[:], d_cT, w=[CT]); dma(MK[:], d_mk, w=[MK])
    op("pool", lambda e: e.memset(ident[:], 0.0), w=[ident])
    op("pool", lambda e: e.affine_select(out=ident[:], in_=ident[:], pattern=[[-1, 128]], compare_op=ALU.not_equal,
                                         fill=1.0, base=0, channel_multiplier=1), r=[ident], w=[ident])
    op("pool", lambda e: e.memset(ones[:], 1.0), w=[ones])
    op("pool", lambda e: e.memset(zeros[:], 0.0), w=[zeros])
    op("pool", lambda e: e.memset(negones[:], -1.0), w=[negones])
    op("pool", lambda e: e.memset(U[:], 1.0), w=[U])
    op("pool", lambda e: e.affine_select(out=U[:], in_=U[:], pattern=[[1, 128]], compare_op=ALU.is_ge,
                                         fill=0.0, base=0, channel_multiplier=-1), r=[U], w=[U])
    op("pool", lambda e: e.memset(MLS[:], 1.0), w=[MLS])
    op("pool", lambda e: e.affine_select(out=MLS[:], in_=MLS[:], pattern=[[-1, 128]], compare_op=ALU.is_gt,
                                         fill=0.0, base=0, channel_multiplier=1), r=[MLS], w=[MLS])
    op("pool", lambda e: e.iota(ramp[:], pattern=[[1, TT]], base=1, channel_multiplier=0,
                                allow_small_or_imprecise_dtypes=True), w=[ramp])
    for t_ in (SST, CCA, CCB, CLRU, CS5, BG):
        op("pool", lambda e, t_=t_: e.memset(t_[:], 0.0), w=[t_])
    for w_ in WK:
        op("pool", lambda e, w_=w_: e.memset(w_[:], 0.0), w=[w_])

    A(SC[:], CT[:], AF.Silu, r=[CT], w=[SC])
    mwi = 0
    for l in range(4):
        for t in range(24):
            mw = MW[mwi % 2]; mwi += 1
            dma(mw[:], d_modw[l, :, :, t * 128:(t + 1) * 128], w=[mw])
            pb = nextps()
            for dc in range(8):
                mm(pb[:, 0:17], mw[:, dc, :], SC[:, dc, :], dc == 0, dc == 7, r=[mw, SC], w=[pb])
            A(MODT[:, l, t, :], pb[:, 0:17], AF.Identity, r=[pb, SP], w=[(MODT, l)], bias=spcol("mod_b", l * 24 + t))

    g0, g1, g2, g3 = GT[0], GT[1], GT[2], GT[3]

    def ln1p_series(dT, eT, n):
        za, wa = GT[4], GT[5]
        d, e_ap = dT[:, 0:n], eT[:, 0:n]
        ts("dve", za[:, 0:n], e_ap, 2.0, ADD, r=[eT], w=[za])
        op("dve", lambda e: e.reciprocal(out=za[:, 0:n], in_=za[:, 0:n]), r=[za], w=[za])
        tt("dve", za[:, 0:n], za[:, 0:n], e_ap, MUL, r=[za, eT], w=[za])
        tt("dve", wa[:, 0:n], za[:, 0:n], za[:, 0:n], MUL, r=[za], w=[wa])
        ts("dve", d, wa[:, 0:n], 1.0 / 13.0, MUL, r=[wa], w=[dT])
        for cst in (1.0 / 11, 1.0 / 9, 1.0 / 7, 1.0 / 5, 1.0 / 3):
            stt(d, d, cst, wa[:, 0:n], ADD, MUL, r=[dT, wa], w=[dT])
        stt(d, d, 1.0, za[:, 0:n], ADD, MUL, r=[dT, za], w=[dT])
        ts("dve", d, d, 2.0, MUL, r=[dT], w=[dT])

    A(g0[:, 0:16], spc("a_param"), AF.Exp, r=[SP], w=[g0], scale=-1.0)
    ln1p_series(g1, g0, 16)
    ts("dve", C1[:].rearrange("p a b -> p (a b)"), g1[:, 0:16], -8.0, MUL, r=[g1], w=[C1])
    ts("dve", C2[:].rearrange("p a b -> p (a b)"), g1[:, 0:16], -16.0, MUL, r=[g1], w=[C2])

    def s5p(i):
        return S5P[:, i, :]
    are, aim, ldt = spc("s5_are"), spc("s5_aim"), spc("s5_ldt")
    A(s5p(0), ldt, AF.Exp, r=[SP], w=[S5P])
    tt("dve", s5p(9), are, s5p(0), MUL, r=[SP, S5P], w=[S5P])
    A(s5p(1), s5p(9), AF.Exp, r=[S5P], w=[S5P])
    tt("dve", s5p(9), aim, s5p(0), MUL, r=[SP, S5P], w=[S5P])
    ts("dve", s5p(10), s5p(9), 1.0 / TWO_PI, MUL, r=[S5P], w=[S5P])
    ts("dve", KI[:, 0:64], s5p(10), 1.0, MUL, r=[S5P], w=[KI])
    cp("dve", s5p(11), KI[:, 0:64], r=[KI], w=[S5P])
    tt("dve", s5p(8), s5p(10), s5p(11), SUB, r=[S5P], w=[S5P])
    A(s5p(3), s5p(8), AF.Sin, r=[S5P], w=[S5P], scale=TWO_PI)
    A(s5p(9), s5p(8), AF.Sin, r=[S5P], w=[S5P], scale=TWO_PI / 2)
    tt("dve", s5p(9), s5p(9), s5p(9), MUL, r=[S5P], w=[S5P])
    ts("dve", s5p(2), s5p(9), -2.0, MUL, r=[S5P], w=[S5P], s2=1.0, o1=ADD)
    tt("dve", s5p(4), s5p(1), s5p(2), MUL, r=[S5P], w=[S5P])
    tt("dve", s5p(5), s5p(1), s5p(3), MUL, r=[S5P], w=[S5P])
    tt("dve", s5p(9), are, are, MUL, r=[SP, S5P], w=[S5P])
    tt("dve", s5p(10), aim, aim, MUL, r=[SP, S5P], w=[S5P])
    tt("dve", s5p(9), s5p(9), s5p(10), ADD, r=[S5P], w=[S5P])
    op("dve", lambda e: e.reciprocal(out=s5p(9), in_=s5p(9)), r=[S5P], w=[S5P])
    ts("dve", s5p(10), s5p(4), -1.0, ADD, r=[S5P], w=[S5P])
    tt("dve", s5p(6), s5p(10), are, MUL, r=[S5P, SP], w=[S5P])
    tt("dve", s5p(11), s5p(5), aim, MUL, r=[S5P, SP], w=[S5P])
    tt("dve", s5p(6), s5p(6), s5p(11), ADD, r=[S5P], w=[S5P])
    tt("dve", s5p(6), s5p(6), s5p(9), MUL, r=[S5P], w=[S5P])
    tt("dve", s5p(7), s5p(5), are, MUL, r=[S5P, SP], w=[S5P])
    tt("dve", s5p(11), s5p(10), aim, MUL, r=[S5P, SP], w=[S5P])
    tt("dve", s5p(7), s5p(7), s5p(11), SUB, r=[S5P], w=[S5P])
    tt("dve", s5p(7), s5p(7), s5p(9), MUL, r=[S5P], w=[S5P])
    ts("dve", s5p(10), s5p(7), -1.0, MUL, r=[S5P], w=[S5P])

    def s5c(i, j, gp):
        return S5P[:, i, j * 32 + gp:j * 32 + gp + 1]

    PBLK = [(b * 512, 512) for b in range(TT // 512)]
    TWO_PI_S = 6.283185
    owi = [0]; wi = [0]

    def scan(out, d0, d1, init, r, w):
        op("dve", lambda e: e.tensor_tensor_scan(out=out, data0=d0, data1=d1, initial=init, op0=MUL, op1=ADD), r=r, w=w)

    def recip(ap, t):
        op("dve", lambda e: e.reciprocal(out=ap, in_=ap), r=[t], w=[t])

    def rms_stats(ncols, blocks):
        pbs = [nextps() for _ in blocks]
        for c in range(8):
            sq = WK[c % 2]
            A(sq[:, 0:ncols], X[:, c, 0:ncols], AF.Square, r=[(X, c)], w=[sq])
            for pb, (bo, bn) in zip(pbs, blocks):
                mm(pb[:, 0:bn], ones[:], sq[:, bo:bo + bn], c == 0, c == 7, r=[ones, sq], w=[pb])
        for pb, (bo, bn) in zip(pbs, blocks):
            A(RS[:, bo:bo + bn], pb[:, 0:bn], AF.Sqrt, r=[pb], w=[RS], bias=EPS, scale=1.0 / 1024)
        recip(RS[:, 0:ncols], RS)

    def layer_start(l, first, ncols, blocks):
        for c in range(8):
            stt(A1[:, c, :], MODT[:, l, 8 + c, :], 1.0, spcol("norm_w", l * 8 + c).to_broadcast([128, 17]), ADD, MUL,
                r=[(MODT, l), SP], w=[A1])
        rms_stats(ncols, blocks)
        for c in range(8):
            tmp = WK[2 + c % 2]
            tt("dve", tmp[:, 0:ncols], X[:, c, 0:ncols], RS[:, 0:ncols], MUL, r=[(X, c), RS], w=[tmp])
            A(H[:, c, 0:TT], tmp[:, 0:TT], AF.Identity, r=[tmp, A1, (MODT, l)], w=[(H, c)],
              scale=A1[:, c, 0:1], bias=MODT[:, l, c, 0:1])
            if first:
                tt("dve", tmp[:, TT:TW], tmp[:, TT:TW], A1[:, c, 1:17], MUL, r=[tmp, A1], w=[tmp])
                tt("dve", H[:, c, TT:TW], tmp[:, TT:TW], MODT[:, l, c, 1:17], ADD, r=[tmp, (MODT, l)], w=[(H, c)])

    PRE = {}

    def prefetch(dsrc, key):
        wt = Wt[wi[0] % 4]; wi[0] += 1
        dmac(wt[:], dsrc, wt, lambda m: m[:])
        PRE[key] = wt

    def proj(dsrc, blocks, rhsT=None, ncolw=128, key=None):
        if key is not None and key in PRE:
            wt = PRE.pop(key)
        else:
            wt = Wt[wi[0] % 4]; wi[0] += 1
            dmac(wt[:, :, 0:ncolw], dsrc, wt, lambda m: m[:, :, 0:ncolw])
        outs = []
        for (bo, bn) in blocks:
            pb = nextps()
            for dc in range(8):
                if rhsT is None:
                    mm(pb[:, 0:bn], wt[:, dc, :], H[:, dc, bo:bo + bn], dc == 0, dc == 7, r=[wt, (H, dc)], w=[pb])
                else:
                    mm(pb[:, 0:bn], wt[:, dc, :], rhsT[:, dc, bo:bo + bn], dc == 0, dc == 7, r=[wt, (rhsT, dc)], w=[pb])
            outs.append((pb, bo, bn))
        return outs

    def out_group(dW, fbase, l, blocks, ybase):
        for m in range(8):
            ow = OW[owi[0] % 2]; owi[0] += 1
            dmac(ow[:], dW[:, fbase:fbase + 4, m * 128:(m + 1) * 128], ow, lambda t_: t_[:, 0:4, :])
            for (bo, bn) in blocks:
                pb = nextps()
                for f in range(4):
                    mm(pb[:, 0:bn], ow[:, f, :], YB[:, ybase + f, bo:bo + bn], f == 0, f == 3,
                       r=[ow, (YB, ybase + f)], w=[pb])
                if bo < TT:
                    stt(X[:, m, bo:bo + bn], pb[:, 0:bn], MODT[:, l, 16 + m, 0:1], X[:, m, bo:bo + bn], MUL, ADD,
                        r=[pb, (MODT, l), (X, m)], w=[(X, m)])
                else:
                    tt("dve", RS[:, 0:16], pb[:, 0:16], MODT[:, l, 16 + m, 1:17], MUL, r=[pb, (MODT, l)], w=[RS])
                    tt("pool", X[:, m, TT:TW], X[:, m, TT:TW], RS[:, 0:16], ADD, r=[RS, (X, m)], w=[(X, m)])

    smi = [0]
    stg = [0]

    def dmac(dst_ap, src, dstT, sel):
        mw = MW[stg[0] % 2]; stg[0] += 1
        dma(sel(mw), src, w=[mw])
        cp("act", dst_ap, sel(mw), r=[mw], w=[dstT])

    def conv_unit(pouts, raw, dst, wname, widx, bias_ap, carry_ap, carry_T, d_state, o_p, o_s, first, last):
        cp("pool", raw[:, 0:3], carry_ap, r=[carry_T], w=[raw])
        for (pb, bo, bn) in pouts:
            cp("act", raw[:, 3 + bo:3 + bo + bn], pb[:, 0:bn], r=[pb], w=[raw])

        def wcol(k):
            return spcol(wname, widx * 4 + k)
        bz = 0.0 if bias_ap is None else bias_ap
        A(dst[:, 0:TT], raw[:, 3:3 + TT], AF.Identity, r=[raw, SP], w=[dst], scale=wcol(3), bias=bz)
        for k in range(3):
            stt(dst[:, 0:TT], raw[:, k:k + TT], wcol(k), dst[:, 0:TT], MUL, ADD, r=[raw, SP, dst], w=[dst])
        cp("pool", carry_ap, raw[:, TT:TT + 3], r=[raw], w=[carry_T])
        if last:
            dma(o_p, raw[:, TT:TT + 3], r=[raw], is_out=True)
        if first:
            st = SMALL[smi[0] % 4]; ns = SMALL[(smi[0] + 1) % 4]; smi[0] += 2
            dma(st[:], d_state, w=[st])
            rawS = raw[:, 3 + TT:3 + TW]
            A(dst[:, TT:TW], rawS, AF.Identity, r=[raw, SP], w=[dst], scale=wcol(3), bias=bz)
            for k in range(3):
                stt(dst[:, TT:TW], st[:, :, k], wcol(k), dst[:, TT:TW], MUL, ADD, r=[st, SP, dst], w=[dst])
            cp("pool", ns[:, :, 0:2], st[:, :, 1:3], r=[st], w=[ns])
            cp("pool", ns[:, :, 2], rawS, r=[raw, ns], w=[ns])
            dma(o_s, ns[:], r=[ns], is_out=True)

    def mark(name):
        S.marks.append((name, dict(S.cnt)))

    def ab_layer(j, seg):
        mark("ab%d_s%d_start" % (j, seg))
        l = 2 * j
        first, last = seg == 0, seg == NSEG - 1
        ncols = TW if first else TT
        blocks = PBLK + ([(TT, 16)] if first else [])
        layer_start(l, first, ncols, blocks)
        dab = d_abin[j]
        import os
        _kab = int(os.environ.get("KAB", "99"))
        if _kab <= 1:
            return
        op("pool", lambda e: e.memset(YB[:, 11, TW - 1:TW], 0.0), w=[YB])
        ylr = [View(YB[:, 4 + 2 * i_:6 + 2 * i_, :].bitcast(F32).rearrange("p a b -> p (a b)"), (YB, ("lru", i_)))
               for i_ in range(2)]
        LSETS = [(WK[0], WK[1], WK[2], WK[3], WK[4]), (WK[5], WK[6], WK[7], ylr[0], ylr[1])]

        def lru_unit(c, W):
            raw, xc, tB, tC, tD = W
            pxa = proj(dab[:, :, c * 128:(c + 1) * 128], blocks, key=("w", c * 128))
            prefetch(dab[:, :, 1024 + c * 128:1024 + (c + 1) * 128], ("w", 1024 + c * 128))
            conv_unit(pxa, raw, xc, "conv_a_w", j * 8 + c, spcol("conv_a_b", j * 8 + c), CCA[:, j, c, :], CCA,
                      d_sca[:, j, c], o_ca[:, j, c, 0, :], o_ca[:, j, c, 1:17, :], first, last)
            yield
            gw = GW[c % 2]
            dma(gw[:, 0, :], d_gxw[j, :, c, :], w=[gw]); dma(gw[:, 1, :], d_gaw[j, :, c, :], w=[gw])
            for (bo, bn) in blocks:
                pb = nextps(); mm(pb[:, 0:bn], gw[:, 0, :], xc[:, bo:bo + bn], True, True, r=[gw, xc], w=[pb])
                A(tB[:, bo:bo + bn], pb[:, 0:bn], AF.Sigmoid, r=[pb, SP], w=[tB], bias=spcol("gx_b", j * 8 + c))
                pb = nextps(); mm(pb[:, 0:bn], gw[:, 1, :], xc[:, bo:bo + bn], True, True, r=[gw, xc], w=[pb])
                A(tC[:, bo:bo + bn], pb[:, 0:bn], AF.Sigmoid, r=[pb, SP], w=[tC], bias=spcol("ga_b", j * 8 + c))
                yield
            A(tD[:, 0:ncols], tC[:, 0:ncols], AF.Exp, r=[tC, C1], w=[tD], scale=C1[:, j, c:c + 1]); yield
            A(tC[:, 0:ncols], tC[:, 0:ncols], AF.Exp, r=[tC, C2], w=[tC], scale=C2[:, j, c:c + 1]); yield
            A(tC[:, 0:ncols], tC[:, 0:ncols], AF.Sqrt, r=[tC], w=[tC], scale=-1.0, bias=1.0); yield
            if first:
                op("pool", lambda e: e.memset(tC[:, 0:1], 1.0), r=[tC], w=[tC])
            tt("dve", tB[:, 0:ncols], tB[:, 0:ncols], tC[:, 0:ncols], MUL, r=[tB, tC], w=[tB]); yield
            tt("dve", tB[:, 0:ncols], tB[:, 0:ncols], xc[:, 0:ncols], MUL, r=[tB, xc], w=[tB]); yield
            scan(xc[:, 0:TT], tD[:, 0:TT], tB[:, 0:TT], CLRU[:, j, c:c + 1], r=[tD, tB, CLRU], w=[xc]); yield
            cp("pool", CLRU[:, j, c:c + 1], xc[:, TT - 1:TT], r=[xc], w=[CLRU])
            if last:
                dma(o_lru[:, j, c, 0:1], xc[:, TT - 1:TT], r=[xc], is_out=True, allow_slow_non_contiguous=True)
            if first:
                tt("pool", xc[:, TT:TW], tD[:, TT:TW], LR0[:, c, :], MUL, r=[tD, LR0, xc], w=[xc])
                tt("pool", xc[:, TT:TW], xc[:, TT:TW], tB[:, TT:TW], ADD, r=[xc, tB], w=[xc])
                dma(o_lru[:, j, c, 1:17], xc[:, TT:TW], r=[xc], is_out=True)
            yield
            pza = proj(dab[:, :, 1024 + c * 128:1024 + (c + 1) * 128], blocks, key=("w", 1024 + c * 128))
            if c + 2 < 8:
                prefetch(dab[:, :, (c + 2) * 128:(c + 3) * 128], ("w", (c + 2) * 128))
            for (pb, bo, bn) in pza:
                A(tD[:, bo:bo + bn], pb[:, 0:bn], AF.Silu, r=[pb], w=[tD])
            yield
            tt("dve", YB[:, c % 4, 0:ncols], xc[:, 0:ncols], tD[:, 0:ncols], MUL, r=[xc, tD], w=[(YB, c % 4)]); yield

        if first:
            dma(LR0[:], d_slru[:, j], w=[LR0])
        for c0 in range(0, 8, 2):
            run_interleaved([lru_unit(c0 + q_, LSETS[q_]) for q_ in range(2)])
            if c0 % 4 == 2:
                out_group(d_about[j], (c0 // 4) * 4, l, blocks, 0)

        if _kab <= 2:
            return
        mark("ab%d_s%d_gdn" % (j, seg))
        nt = NCH + (1 if first else 0)
        n8 = nt * 8
        wt = Wt[wi[0] % 4]; wi[0] += 1
        dmac(wt[:, :, 0:16], dab[:, :, 6144:6160], wt, lambda m: m[:, :, 0:16])
        pb = nextps()
        for t in range(NCH):
            for dc in range(8):
                mm(pb[:, t * 16:(t + 1) * 16], H[:, dc, t * 128:(t + 1) * 128], wt[:, dc, 0:16], dc == 0, dc == 7,
                   r=[wt, (H, dc)], w=[pb])
        cp("act", BG[:, 0:NCH, :].rearrange("p a b -> p (a b)"), pb[:, 0:NCH * 16], r=[pb], w=[BG])
        if first:
            for dc in range(8):
                mm(pb[0:16, NCH * 16:(NCH + 1) * 16], H[:, dc, TT:TW], wt[:, dc, 0:16], dc == 0, dc == 7,
                   r=[wt, (H, dc)], w=[pb])
            cp("act", BG[0:16, NCH, :], pb[0:16, NCH * 16:(NCH + 1) * 16], r=[pb], w=[BG])

        def v3(t_):
            return t_[:, 0:n8].rearrange("p (a b) -> p a b", b=8)
        cp("pool", v3(GT[0]), BG[:, 0:nt, 0:8], r=[BG], w=[GT[0]])
        cp("pool", v3(GT[1]), BG[:, 0:nt, 8:16], r=[BG], w=[GT[1]])
        btm2 = BTM[:, 0:nt, :].rearrange("p a b -> p (a b)")
        gtm2 = GTM[:, 0:nt, :].rearrange("p a b -> p (a b)")
        A(btm2, GT[0][:, 0:n8], AF.Sigmoid, r=[GT[0]], w=[BTM])
        tt("dve", GT[1][:, 0:n8], GT[1][:, 0:n8], spc("dt_bias", j * 72, j * 72 + n8), ADD, r=[GT[1], SP], w=[GT[1]])
        A(GT[0][:, 0:n8], GT[1][:, 0:n8], AF.Abs, r=[GT[1]], w=[GT[0]])
        A(GT[0][:, 0:n8], GT[0][:, 0:n8], AF.Exp, r=[GT[0]], w=[GT[0]], scale=-1.0)
        ln1p_series(GT[2], GT[0], n8)
        A(GT[1][:, 0:n8], GT[1][:, 0:n8], AF.Relu, r=[GT[1]], w=[GT[1]])
        tt("dve", GT[2][:, 0:n8], GT[2][:, 0:n8], GT[1][:, 0:n8], ADD, r=[GT[2], GT[1]], w=[GT[2]])
        A(GT[3][:, 0:n8], spc("a_log", j * 72, j * 72 + n8), AF.Exp, r=[SP], w=[GT[3]])
        tt("dve", GT[3][:, 0:n8], GT[3][:, 0:n8], GT[2][:, 0:n8], MUL, r=[GT[3], GT[2]], w=[GT[3]])
        ts("dve", gtm2, GT[3][:, 0:n8], -1.0, MUL, r=[GT[3]], w=[GTM])
        pb = nextps()
        for t in range(NCH):
            mm(pb[:, t * 16:t * 16 + 8], U[:], GTM[:, t, :], True, True, r=[U, GTM], w=[pb])
            mm(pb[:, t * 16 + 8:t * 16 + 16], ones[:], GTM[:, t, :], True, True, r=[ones, GTM], w=[pb])
        cp("act", GCS[:].rearrange("p a b -> p (a b)"), pb[:, 0:NCH * 16], r=[pb], w=[GCS])
        A(EGC[:], GCS[:, :, 0:8], AF.Exp, r=[GCS], w=[EGC])
        tt("dve", BEGE[:], EGC[:], BTM[:, 0:NCH, :], MUL, r=[EGC, BTM], w=[BEGE])
        tt("dve", EDEC[:], GCS[:, :, 8:16], GCS[:, :, 0:8], SUB, r=[GCS], w=[EDEC])
        A(EDEC[:], EDEC[:], AF.Exp, r=[EDEC], w=[EDEC])
        A(EGL[:], GCS[:, :, 8:16], AF.Exp, r=[GCS], w=[EGL])
        if first:
            A(GT[4][0:16, 0:8], GTM[0:16, NCH, :], AF.Exp, r=[GTM], w=[GT[4]])
            for hh in range(8):
                ts("pool", RM[0:16, 0, hh, :], ident[0:16, 0:16], GT[4][0:16, hh:hh + 1], MUL, r=[ident, GT[4]], w=[RM])
                ts("pool", RM[0:16, 1, hh, :], ident[0:16, 0:16], BTM[0:16, NCH, hh:hh + 1], MUL, r=[ident, BTM], w=[RM])
            pb = nextps()
            mm(pb[:, 0:256], ones[0:16, :], RM[0:16].rearrange("p a b c -> p (a b c)"), True, True, r=[ones, RM], w=[pb])
            cp("act", EGB[:].rearrange("p a b c -> p (a b c)"), pb[:, 0:256], r=[pb], w=[EGB])

        if _kab <= 3:
            return
        op("pool", lambda e: e.memset(YB[:, 11, TW - 1:TW], 0.0), w=[YB])
        op("pool", lambda e: e.memset(WK[7][:, TW + 2:TW + 3], 0.0), w=[WK[7]])
        _kcl = int(os.environ.get("KCL", "99"))
        _kch = int(os.environ.get("KCHUNK", "1")); _ksm = int(os.environ.get("KSAMP", "1"))
        for hh in range(int(os.environ.get("KHEADS", "8"))):
            raw, Q, K, V, Z, T1, OT = WK[0], WK[1], WK[2], WK[3], WK[4], WK[5], WK[6]

            op("pool", lambda e: e.memset(YB[:, 11, TW - 1:TW], 0.0), w=[YB])
            ysu = YB[:, 4:12, :].bitcast(F32).rearrange("p a b -> p (a b)")
            SU = [View(ysu[:, i_ * (TW + 3):(i_ + 1) * (TW + 3)], (YB, ("su", i_))) for i_ in range(3)]
            raw_k, raw_v, T1k = SU

            def colnorm_g(dst, tmp, scale_sum, post):
                tt("dve", tmp[:, 0:ncols], dst[:, 0:ncols], dst[:, 0:ncols], MUL, r=[dst], w=[tmp]); yield
                for (bo, bn) in blocks:
                    pb = nextps(); mm(pb[:, 0:bn], ones[:], tmp[:, bo:bo + bn], True, True, r=[ones, tmp], w=[pb])
                    A(tmp[:, bo:bo + bn], pb[:, 0:bn], AF.Sqrt, r=[pb], w=[tmp], bias=EPS, scale=scale_sum); yield
                recip(tmp[:, 0:ncols], tmp); yield
                if post is None:
                    tt("dve", dst[:, 0:ncols], dst[:, 0:ncols], tmp[:, 0:ncols], MUL, r=[dst, tmp], w=[dst]); yield
                else:
                    stt(dst[:, 0:ncols], dst[:, 0:ncols], post, tmp[:, 0:ncols], MUL, MUL, r=[dst, tmp], w=[dst]); yield

            def chain_qkv(col0, unit, dst, rawT, tmp, post, do_norm):
                p = proj(dab[:, :, col0:col0 + 128], blocks, key=("w", col0))
                conv_unit(p, rawT, dst, "conv_b_w", j * 24 + unit, None, CCB[:, j, unit, :], CCB,
                          d_scb[:, j, unit], o_cb[:, j, unit, 0, :], o_cb[:, j, unit, 1:17, :], first, last)
                yield
                A(dst[:, 0:ncols], dst[:, 0:ncols], AF.Silu, r=[dst], w=[dst]); yield
                if do_norm:
                    yield from colnorm_g(dst, tmp, 1.0, post)

            def chain_z():
                pz = proj(dab[:, :, 5120 + hh * 128:5120 + (hh + 1) * 128], blocks, key=("w", 5120 + hh * 128))
                for (pb, bo, bn) in pz:
                    A(Z[:, bo:bo + bn], pb[:, 0:bn], AF.Silu, r=[pb], w=[Z])
                yield

            def colnorm(dst, scale_sum, post):
                for _ in colnorm_g(dst, T1, scale_sum, post):
                    pass

            run_interleaved([chain_qkv(2048 + hh * 128, hh, Q, raw, T1, 128.0 ** -0.5, True),
                             chain_qkv(3072 + hh * 128, 8 + hh, K, raw_k, T1k, None, True),
                             chain_qkv(4096 + hh * 128, 16 + hh, V, raw_v, None, None, False),
                             chain_z()])
            if hh + 1 < 8:
                prefetch(dab[:, :, 2048 + (hh + 1) * 128:2048 + (hh + 2) * 128], ("w", 2048 + (hh + 1) * 128))
                prefetch(dab[:, :, 3072 + (hh + 1) * 128:3072 + (hh + 2) * 128], ("w", 3072 + (hh + 1) * 128))
            op("pool", lambda e: e.memset(YB[:, 11, TW - 1:TW], 0.0), w=[YB])

            SK = (SST, (j, hh))

            def chunk_par(t, c_, psn):
                sl = slice(t * 128, (t + 1) * 128)
                bcol = BTM[:, t, hh:hh + 1]
                pa = psn()
                op("pe", lambda e: e.transpose(pa[:, 0:128], K[:, sl], ident[:]), r=[K, ident], w=[pa]); yield
                A(c_["kbg"][:], pa[:, 0:128], AF.Identity, r=[pa, BEGE], w=[c_["kbg"]], scale=BEGE[:, t, hh:hh + 1]); yield
                A(c_["kdec"][:], pa[:, 0:128], AF.Identity, r=[pa, EDEC], w=[c_["kdec"]], scale=EDEC[:, t, hh:hh + 1]); yield
                pv = psn()
                op("pe", lambda e: e.transpose(pv[:, 0:128], V[:, sl], ident[:]), r=[V, ident], w=[pv]); yield
                A(c_["vb"][:], pv[:, 0:128], AF.Identity, r=[pv, BTM], w=[c_["vb"]], scale=bcol); yield
                ts("pool", c_["gbc"][:], ones[:], GTM[:, t, hh:hh + 1], MUL, r=[ones, GTM], w=[c_["gbc"]]); yield
                pe_ = psn(); mm(pe_[:, 0:128], c_["gbc"][:], U[:], True, True, r=[c_["gbc"], U], w=[pe_]); yield
                gcc = GCS[:, t, hh:hh + 1]
                cp("act", c_["gcr"][:], pe_[:, 0:128], r=[pe_], w=[c_["gcr"]]); yield
                stt(c_["E1"][:], c_["gcr"][:], gcc, zeros[:], SUB, ALU.max, r=[c_["gcr"], GCS, zeros], w=[c_["E1"]]); yield
                stt(c_["E1T"][:], c_["gcr"][:], gcc, zeros[:], SUB, ALU.min, r=[c_["gcr"], GCS, zeros], w=[c_["E1T"]]); yield
                A(c_["decL"][:], c_["E1"][:], AF.Exp, r=[c_["E1"]], w=[c_["decL"]], scale=-1.0); yield
                tt("pool", c_["decL"][:], c_["decL"][:], MLS[:], MUL, r=[c_["decL"], MLS], w=[c_["decL"]]); yield
                pc = psn(); mm(pc[:, 0:128], K[:, sl], K[:, sl], True, True, r=[K], w=[pc]); yield
                stt(c_["A0"][:], pc[:, 0:128], bcol, c_["decL"][:], MUL, MUL, r=[pc, BTM, c_["decL"]], w=[c_["A0"]]); yield
                A(c_["decU"][:], c_["E1T"][:], AF.Exp, r=[c_["E1T"]], w=[c_["decU"]]); yield
                tt("pool", c_["decU"][:], c_["decU"][:], U[:], MUL, r=[c_["decU"], U], w=[c_["decU"]]); yield
                pd = psn(); mm(pd[:, 0:128], K[:, sl], Q[:, sl], True, True, r=[K, Q], w=[pd]); yield
                tt("dve", c_["QKmT"][:], pd[:, 0:128], c_["decU"][:], MUL, r=[pd, c_["decU"]], w=[c_["QKmT"]]); yield
                A(c_["egr"][:], c_["gcr"][:], AF.Exp, r=[c_["gcr"]], w=[c_["egr"]]); yield
                tt("pool", c_["qg"][:], Q[:, sl], c_["egr"][:], MUL, r=[Q, c_["egr"]], w=[c_["qg"]]); yield
                pf = psn()
                op("pe", lambda e: e.transpose(pf[:, 0:128], c_["A0"][:], ident[:]), r=[c_["A0"], ident], w=[pf]); yield
                cp("act", c_["B0"][:], pf[:, 0:128], r=[pf], w=[c_["B0"]]); yield
                A16, B16 = c_["Aa"], c_["Ba"]
                tt("pool", A16[:], c_["A0"][:], MK[:, 0, :], MUL, r=[c_["A0"], MK], w=[A16]); yield
                tt("pool", B16[:], c_["B0"][:], MK[:, 0, :], MUL, r=[c_["B0"], MK], w=[B16]); yield
                Pp, Rp = c_["P0"], c_["R0"]
                tt("dve", Pp[:], ident[:], B16[:], SUB, r=[B16, ident], w=[Pp]); yield
                tt("dve", Rp[:], ident[:], A16[:], SUB, r=[A16, ident], w=[Rp]); yield
                Ap, Bp = A16, B16
                Abuf, Bbuf, Pbuf, Rbuf = [c_["Ab"], c_["Aa"]], [c_["Bb"], c_["Ba"]], [c_["P1"], c_["P0"]], [c_["R1"], c_["R0"]]
                for k in range(1, 4):
                    pg = psn(); mm(pg[:, 0:128], Bp[:], Ap[:], True, True, r=[Bp, Ap], w=[pg]); yield
                    An = Abuf[(k - 1) % 2]; cp("act", An[:], pg[:, 0:128], r=[pg], w=[An]); yield
                    ph = psn(); mm(ph[:, 0:128], Ap[:], Bp[:], True, True, r=[Bp, Ap], w=[ph]); yield
                    Bn = Bbuf[(k - 1) % 2]; cp("dve", Bn[:], ph[:, 0:128], r=[ph], w=[Bn]); yield
                    pi_ = psn(); mm(pi_[:, 0:128], An[:], Pp[:], True, True, r=[An, Pp], w=[pi_]); yield
                    Pn = Pbuf[(k - 1) % 2]
                    tt("dve", Pn[:], Pp[:], pi_[:, 0:128], ADD, r=[Pp, pi_], w=[Pn]); yield
                    pq = psn(); mm(pq[:, 0:128], Bn[:], Rp[:], True, True, r=[Bn, Rp], w=[pq]); yield
                    Rn = Rbuf[(k - 1) % 2]
                    tt("dve", Rn[:], Rp[:], pq[:, 0:128], ADD, r=[Rp, pq], w=[Rn]); yield
                    Ap, Bp, Pp, Rp = An, Bn, Pn, Rn
                for m in range(1, 4):
                    AM, BM, Xa, Xb = c_["Aa"], c_["Ba"], c_["Ab"], c_["Bb"]
                    tt("pool", AM[:], c_["A0"][:], MK[:, m, :], MUL, r=[c_["A0"], MK], w=[AM]); yield
                    px = psn(); mm(px[:, 0:128], AM[:], Pp[:], True, True, r=[AM, Pp], w=[px]); yield
                    cp("act", Xa[:], px[:, 0:128], r=[px], w=[Xa]); yield
                    py = psn(); mm(py[:, 0:128], Rp[:], Xa[:], True, True, r=[Rp, Xa], w=[py]); yield
                    Pn = c_["P0"] if Pp is c_["P1"] else c_["P1"]
                    tt("dve", Pn[:], Pp[:], py[:, 0:128], SUB, r=[Pp, py], w=[Pn]); yield
                    if m < 3:
                        tt("pool", BM[:], c_["B0"][:], MK[:, m, :], MUL, r=[c_["B0"], MK], w=[BM]); yield
                        px2 = psn(); mm(px2[:, 0:128], BM[:], Rp[:], True, True, r=[BM, Rp], w=[px2]); yield
                        cp("act", Xb[:], px2[:, 0:128], r=[px2], w=[Xb]); yield
                        py2 = psn(); mm(py2[:, 0:128], Pp[:], Xb[:], True, True, r=[Pp, Xb], w=[py2]); yield
                        Rn = c_["R0"] if Rp is c_["R1"] else c_["R1"]
                        tt("dve", Rn[:], Rp[:], py2[:, 0:128], SUB, r=[Rp, py2], w=[Rn]); yield
                        Rp = Rn
                    Pp = Pn
                TTt = Pp
                pj = psn(); mm(pj[:, 0:128], TTt[:], c_["vb"][:], True, True, r=[TTt, c_["vb"]], w=[pj]); yield
                cp("act", c_["u"][:], pj[:, 0:128], r=[pj], w=[c_["u"]]); yield
                pk = psn(); mm(pk[:, 0:128], c_["kbg"][:], TTt[:], True, True, r=[TTt, c_["kbg"]], w=[pk]); yield
                cp("dve", c_["wT"][:], pk[:, 0:128], r=[pk], w=[c_["wT"]]); yield

            def chunk_seq(t, c_, psn):
                sl = slice(t * 128, (t + 1) * 128)
                if t % 2 == 0:
                    Scur, ScurR, Snxt, SnxtR = SST[:, j, hh, :], SK, STMP[:], STMP
                else:
                    Scur, ScurR, Snxt, SnxtR = STMP[:], STMP, SST[:, j, hh, :], SK
                pl = psn(); mm(pl[:, 0:128], c_["wT"][:], Scur, True, True, r=[c_["wT"], ScurR], w=[pl])
                tt("dve", c_["vnew"][:], c_["u"][:], pl[:, 0:128], SUB, r=[c_["u"], pl], w=[c_["vnew"]])
                pm = psn()
                mm(pm[:, 0:128], Scur, c_["qg"][:], True, False, r=[ScurR, c_["qg"]], w=[pm])
                mm(pm[:, 0:128], c_["vnew"][:], c_["QKmT"][:], False, True, r=[c_["vnew"], c_["QKmT"]], w=[pm])
                cp("act", OT[:, sl], pm[:, 0:128], r=[pm], w=[OT])
                pn = psn(); mm(pn[:, 0:128], c_["kdec"][:], c_["vnew"][:], True, True, r=[c_["kdec"], c_["vnew"]], w=[pn])
                stt(Snxt, Scur, EGL[:, t, hh:hh + 1], pn[:, 0:128], MUL, ADD, r=[ScurR, EGL, pn], w=[SnxtR])

            mark("L%d_s%d_h%d_chunks" % (j, seg, hh))
            for t0 in range(0, NCH if _kch else 0, NSTR):
                run_interleaved([chunk_par(t0 + q, CKS[q], PSN[q]) for q in range(NSTR)])
                for q in range(NSTR):
                    chunk_seq(t0 + q, CKS[q], PSN[q])
            if last:
                dma(o_delta[j, 0, hh], SST[:, j, hh, :], r=[SK], is_out=True)
            mark("L%d_s%d_h%d_samples" % (j, seg, hh))
            if first and _ksm:
                save = psrot["lst"]
                po = PS[7]
                psrot["lst"] = [x for x in save if x != 7]
                def sample_step(s, q):
                    s0, sn, dg = hpool[3 * q], hpool[3 * q + 1], hpool[3 * q + 2]
                    pq_ = PS[q]
                    d0, d1 = DFM[:, 2 * q:2 * q + 1], DFM[:, 2 * q + 1:2 * q + 2]
                    kcol = K[:, TT + s:TT + s + 1]
                    dma(s0[:], d_sdelta[j, s, hh], w=[s0]); yield
                    mm(pq_[:, 0:1], s0[:], kcol, True, True, r=[s0, K], w=[pq_]); yield
                    stt(d0, pq_[:, 0:1], EGB[:, 0, hh, s:s + 1], V[:, TT + s:TT + s + 1], MUL, SUB,
                        r=[pq_, EGB, V], w=[(DFM, q)]); yield
                    stt(d1, d0, EGB[:, 1, hh, s:s + 1], negones[:, 0:1], MUL, MUL,
                        r=[(DFM, q), EGB, negones], w=[(DFM, q)]); yield
                    ts("pool", dg[:], ident[:], d1, MUL, r=[ident, (DFM, q)], w=[dg]); yield
                    mm(pq_[:, 128:256], ones[:], dg[:], True, True, r=[ones, dg], w=[pq_]); yield
                    ts("dve", sn[:], s0[:], EGB[:, 0, hh, s:s + 1], MUL, r=[s0, EGB], w=[sn]); yield
                    stt(sn[:], pq_[:, 128:256], kcol, sn[:], MUL, ADD, r=[pq_, K, sn], w=[sn]); yield
                    dma(o_delta[j, 1 + s, hh], sn[:], r=[sn], is_out=True); yield
                    mm(po[:, s:s + 1], sn[:], Q[:, TT + s:TT + s + 1], True, True, r=[sn, Q], w=[po]); yield

                NSS = 6
                for s_ in range(0, 16, NSS):
                    run_interleaved([sample_step(s_ + q, q) for q in range(NSS) if s_ + q < 16])
                cp("act", OT[:, TT:TW], po[:, 0:16], r=[po], w=[OT])
                psrot["lst"] = save
            mark("L%d_s%d_h%d_onorm" % (j, seg, hh))
            colnorm(OT, 1.0 / 128, None)
            stt(YB[:, hh % 4, 0:ncols], OT[:, 0:ncols], spcol("gnorm_w", j), Z[:, 0:ncols], MUL, MUL,
                r=[OT, SP, Z], w=[(YB, hh % 4)])
            if hh % 4 == 3:
                out_group(d_about[j], 8 + (hh // 4) * 4, l, blocks, 0)
        op("pool", lambda e: e.memset(YB[:, 11, TW - 1:TW], 0.0), w=[YB])
        op("pool", lambda e: e.memset(WK[7][:, TW + 2:TW + 3], 0.0), w=[WK[7]])

    def s5_layer(j, seg):
        mark("s5%d_s%d_start" % (j, seg))
        l = 2 * j + 1
        first, last = seg == 0, seg == NSEG - 1
        ncols = TW if first else TT
        blocks = PBLK + ([(TT, 16)] if first else [])
        layer_start(l, first, ncols, blocks)
        nyb = len(PBLK)
        ybanks = [7 - i for i in range(nyb)]
        ysb = 7 - nyb
        save = psrot["lst"]
        psrot["lst"] = [x for x in save if x not in ybanks and x != ysb]
        cosT, sinT, bre, bim, tA, tB, yf, Uf = WK
        Ub = XB[2]
        for c in range(8):
            pu = proj(d_cin[j][:, :, c * 128:(c + 1) * 128], blocks)
            for (pb, bo, bn) in pu:
                cp("act", Uf[:, bo:bo + bn], pb[:, 0:bn], r=[pb], w=[Uf])
            cp("pool", Ub[:, 0:ncols], Uf[:, 0:ncols], r=[Uf], w=[Ub])
            for q in range(4):
                gp = c * 4 + q
                bp = BP[gp % 2]; cr = CR[gp % 2]; cf = CF[gp % 2]
                dmac(bp[:, 0, :], d_s5B[j, 0, gp], bp, lambda m: m[:, 0, :]); dmac(bp[:, 1, :], d_s5B[j, 1, gp], bp, lambda m: m[:, 1, :])
                dma(cr[:, 0, :], d_s5C[j, 0, gp], w=[cr]); dma(cr[:, 1, :], d_s5C[j, 1, gp], w=[cr])
                cre_s, cim_s, ncim_s = s5c(6, j, gp), s5c(7, j, gp), s5c(10, j, gp)
                ts("dve", CTMP[:], cr[:, 1, :], cim_s, MUL, r=[cr, S5P], w=[CTMP])
                stt(cf[:, 0, :], cr[:, 0, :], cre_s, CTMP[:], MUL, SUB, r=[cr, S5P, CTMP], w=[cf])
                ts("dve", CTMP[:], cr[:, 1, :], cre_s, MUL, r=[cr, S5P], w=[CTMP])
                stt(cf[:, 1, :], cr[:, 0, :], ncim_s, CTMP[:], MUL, SUB, r=[cr, S5P, CTMP], w=[cf])
                SB = 512
                NSB = TT // SB

                def tables(g2_, h):
                    th = s5c(8, j, g2_)
                    C_, S_, K_ = cosT[:, h * SB:(h + 1) * SB], sinT[:, h * SB:(h + 1) * SB], KI[:, h * SB:(h + 1) * SB]
                    rc, rs_, rk = (cosT, h), (sinT, h), (KI, h)
                    ts("dve", K_, ramp[:, 0:SB], th, MUL, r=[ramp, S5P], w=[rk]); yield
                    cp("dve", S_, K_, r=[rk], w=[rs_]); yield
                    stt(C_, ramp[:, 0:SB], th, S_, MUL, SUB, r=[ramp, S5P, rs_], w=[rc]); yield
                    A(S_, C_, AF.Sin, r=[rc], w=[rs_], scale=TWO_PI_S); yield
                    A(C_, C_, AF.Sin, r=[rc], w=[rc], scale=TWO_PI_S / 2); yield
                    A(C_, C_, AF.Square, r=[rc], w=[rc]); yield
                    A(C_, C_, AF.Identity, r=[rc], w=[rc], scale=-2.0, bias=1.0); yield

                def chain(sb_, h):
                    sl = slice(sb_ * SB, (sb_ + 1) * SB)
                    c_, s_ = cosT[:, h * SB:(h + 1) * SB], sinT[:, h * SB:(h + 1) * SB]
                    rc, rs_ = (cosT, h), (sinT, h)
                    br, bi_, a_, b_ = bre[:, sl], bim[:, sl], tA[:, sl], tB[:, sl]
                    kb, ki_, ka, kb2 = (bre, sb_), (bim, sb_), (tA, sb_), (tB, sb_)
                    x0, x1 = (XB[0], sb_), (XB[1], sb_)
                    p1 = nextps(); mm(p1[:, 0:SB], bp[:, 0, :], Ub[:, sl], True, True, r=[bp, Ub], w=[p1]); yield
                    cp("act", br, p1[:, 0:SB], r=[p1], w=[kb]); yield
                    p2 = nextps(); mm(p2[:, 0:SB], bp[:, 1, :], Ub[:, sl], True, True, r=[bp, Ub], w=[p2]); yield
                    cp("act", bi_, p2[:, 0:SB], r=[p2], w=[ki_]); yield
                    tt("pool", a_, c_, br, MUL, r=[rc, kb], w=[ka]); yield
                    tt("dve", b_, s_, bi_, MUL, r=[rs_, ki_], w=[kb2]); yield
                    tt("dve", a_, a_, b_, ADD, r=[ka, kb2], w=[ka]); yield
                    tt("pool", b_, c_, bi_, MUL, r=[rc, ki_], w=[kb2]); yield
                    tt("dve", br, s_, br, MUL, r=[rs_, kb], w=[kb]); yield
                    tt("dve", b_, b_, br, SUB, r=[kb2, kb], w=[kb2]); yield
                    rrbc = s5c(1, j, gp).to_broadcast([128, SB])
                    if sb_ == 0:
                        ire, iim, rin = CS5[:, j, 0, gp:gp + 1], CS5[:, j, 1, gp:gp + 1], (CS5, (j, gp))
                    else:
                        ire, iim, rin = CMID[:, sb_ - 1, 0:1], CMID[:, sb_ - 1, 1:2], (CMID, sb_ - 1)
                    if sb_ == NSB - 1:
                        ore, oim, rout = CS5[:, j, 0, gp:gp + 1], CS5[:, j, 1, gp:gp + 1], (CS5, (j, gp))
                    else:
                        ore, oim, rout = CMID[:, sb_, 0:1], CMID[:, sb_, 1:2], (CMID, sb_)
                    scan(bi_, rrbc, a_, ire, r=[S5P, ka, rin], w=[ki_]); yield
                    scan(br, rrbc, b_, iim, r=[S5P, kb2, rin], w=[kb]); yield
                    tt("pool", a_, c_, bi_, MUL, r=[rc, ki_], w=[ka]); yield
                    tt("dve", b_, s_, br, MUL, r=[rs_, kb], w=[kb2]); yield
                    tt("dve", XB[0][:, sl], a_, b_, SUB, r=[ka, kb2], w=[x0]); yield
                    tt("dve", ore, tA[:, (sb_ + 1) * SB - 1:(sb_ + 1) * SB], tB[:, (sb_ + 1) * SB - 1:(sb_ + 1) * SB], SUB,
                       r=[ka, kb2], w=[rout]); yield
                    tt("pool", a_, s_, bi_, MUL, r=[rs_, ki_], w=[ka]); yield
                    tt("dve", b_, c_, br, MUL, r=[rc, kb], w=[kb2]); yield
                    tt("dve", XB[1][:, sl], a_, b_, ADD, r=[ka, kb2], w=[x1]); yield
                    tt("dve", oim, tA[:, (sb_ + 1) * SB - 1:(sb_ + 1) * SB], tB[:, (sb_ + 1) * SB - 1:(sb_ + 1) * SB], ADD,
                       r=[ka, kb2], w=[rout]); yield
                    yb = PS[ybanks[sb_]]
                    mm(yb[:, 0:SB], cf[:, 0, :], XB[0][:, sl], q == 0, False, r=[cf, x0], w=[yb]); yield
                    mm(yb[:, 0:SB], cf[:, 1, :], XB[1][:, sl], False, q == 3, r=[cf, x1], w=[yb]); yield

                hbuf = gp % 2
                if gp == 0:
                    run_interleaved([tables(0, 0)])
                def delayed(g_, n_):
                    for _ in range(n_):
                        yield
                    yield from g_
                gens = [delayed(chain(sb_, hbuf), 9 * sb_) for sb_ in range(NSB)]
                if gp + 1 < 32:
                    gens.append(tables(gp + 1, 1 - hbuf))
                run_interleaved(gens)
                if first:
                    dma(S5S[:, 0, :], d_ss5re[:, j, gp, :], w=[S5S]); dma(S5S[:, 1, :], d_ss5im[:, j, gp, :], w=[S5S])
                    p1 = nextps()
                    mm(p1[:, 0:16], bp[:, 0, :], Ub[:, TT:TW], True, True, r=[bp, Ub], w=[p1])
                    mm(p1[:, 16:32], bp[:, 1, :], Ub[:, TT:TW], True, True, r=[bp, Ub], w=[p1])
                    cp("act", S5T[:].rearrange("p a b -> p (a b)"), p1[:, 0:32], r=[p1], w=[S5T])
                    Are, Aim = s5c(4, j, gp), s5c(5, j, gp)
                    x0r, x0i, pr_, pi2 = S5S[:, 0, :], S5S[:, 1, :], S5T[:, 0, :], S5T[:, 1, :]
                    N0, N1, N2 = S5N[:, 0, :], S5N[:, 1, :], S5N[:, 2, :]
                    ts("dve", S5U[:], x0i, Aim, MUL, r=[S5S, S5P], w=[S5U])
                    stt(N0, x0r, Are, S5U[:], MUL, SUB, r=[S5S, S5P, S5U], w=[S5N])
                    stt(N0, pr_, cre_s, N0, MUL, ADD, r=[S5T, S5P, S5N], w=[S5N])
                    stt(N0, pi2, ncim_s, N0, MUL, ADD, r=[S5T, S5P, S5N], w=[S5N])
                    ts("dve", S5U[:], x0r, Aim, MUL, r=[S5S, S5P], w=[S5U])
                    stt(N1, x0i, Are, S5U[:], MUL, ADD, r=[S5S, S5P, S5U], w=[S5N])
                    stt(N1, pi2, cre_s, N1, MUL, ADD, r=[S5T, S5P, S5N], w=[S5N])
                    stt(N1, pr_, cim_s, N1, MUL, ADD, r=[S5T, S5P, S5N], w=[S5N])
                    ts("dve", N2, N1, -1.0, MUL, r=[S5N], w=[S5N])
                    dma(o_s5re[:, j, gp, :], N0, r=[S5N], is_out=True)
                    dma(o_s5im[:, j, gp, :], N1, r=[S5N], is_out=True)
                    ys = PS[ysb]
                    mm(ys[:, 0:16], cr[:, 0, :], N0, q == 0, False, r=[cr, S5N], w=[ys])
                    mm(ys[:, 0:16], cr[:, 1, :], N2, False, q == 3, r=[cr, S5N], w=[ys])
            dcol = spcol("s5_d", j * 8 + c)
            for bi2, (bo, bn) in enumerate(PBLK):
                yb = PS[ybanks[bi2]]
                stt(yf[:, bo:bo + bn], Uf[:, bo:bo + bn], dcol, yb[:, 0:bn], MUL, ADD, r=[Uf, SP, yb], w=[yf])
            if first:
                stt(yf[:, TT:TW], Uf[:, TT:TW], dcol, PS[ysb][:, 0:16], MUL, ADD, r=[Uf, SP, PS[ysb]], w=[yf])
            A(YB[:, c, 0:ncols], yf[:, 0:ncols], AF.Gelu_apprx_tanh, r=[yf], w=[(YB, c)])
        psrot["lst"] = save
        if last:
            cr_, ci_ = S5P[:, 6, j * 32:(j + 1) * 32], S5P[:, 7, j * 32:(j + 1) * 32]
            xr, xi = CS5[:, j, 0, :], CS5[:, j, 1, :]
            tt("dve", S5O[:, 0, :], ci_, xi, MUL, r=[S5P, CS5], w=[S5O])
            tt("dve", S5O[:, 1, :], cr_, xr, MUL, r=[S5P, CS5], w=[S5O])
            tt("dve", S5O[:, 0, :], S5O[:, 1, :], S5O[:, 0, :], SUB, r=[S5O], w=[S5O])
            dma(o_s5pre[:, j, :], S5O[:, 0, :], r=[S5O], is_out=True)
            tt("dve", S5O[:, 0, :], ci_, xr, MUL, r=[S5P, CS5, S5O], w=[S5O])
            tt("dve", S5O[:, 1, :], cr_, xi, MUL, r=[S5P, CS5], w=[S5O])
            tt("dve", S5O[:, 1, :], S5O[:, 1, :], S5O[:, 0, :], ADD, r=[S5O], w=[S5O])
            dma(o_s5pim[:, j, :], S5O[:, 1, :], r=[S5O], is_out=True)
        mark("s5%d_s%d_glu" % (j, seg))
        sg, zs = WK[0], WK[1]
        for m in range(8):
            pgl = proj(d_gluw[j][:, :, m * 128:(m + 1) * 128], blocks, rhsT=YB)
            for (pb, bo, bn) in pgl:
                A(sg[:, bo:bo + bn], pb[:, 0:bn], AF.Sigmoid, r=[pb, SP], w=[sg], bias=spcol("glu_b", j * 8 + m))
            pz = proj(d_cin[j][:, :, 1024 + m * 128:1024 + (m + 1) * 128], blocks)
            for (pb, bo, bn) in pz:
                A(zs[:, bo:bo + bn], pb[:, 0:bn], AF.Silu, r=[pb], w=[zs])
            tt("pool", sg[:, 0:ncols], sg[:, 0:ncols], zs[:, 0:ncols], MUL, r=[sg, zs], w=[sg])
            tt("dve", YB[:, 8 + m % 4, 0:ncols], sg[:, 0:ncols], YB[:, m, 0:ncols], MUL, r=[sg, (YB, m)], w=[(YB, 8 + m % 4)])
            if m % 4 == 3:
                out_group(d_cout[j], (m // 4) * 4, l, blocks, 8)

    for seg in range(NSEG):
        first = seg == 0
        ncols = TW if first else TT
        blocks = PBLK + ([(TT, 16)] if first else [])
        for c in range(8):
            dma(X[:, c, 0:TT], d_xp[:, c, seg * TT:(seg + 1) * TT], w=[(X, c)])
            if first:
                dma(X[:, c, TT:TW], d_xs[:, c, :], w=[(X, c)])
        import os
        _ks = int(os.environ.get("KSTOP", "99"))
        if _ks >= 1:
            ab_layer(0, seg)
        if _ks >= 2:
            s5_layer(0, seg)
        if _ks >= 3:
            ab_layer(1, seg); s5_layer(1, seg)
        mark("final_s%d" % seg)
        rms_stats(ncols, blocks)
        for c in range(8):
            tmp = WK[2 + c % 2]
            tt("pool", tmp[:, 0:ncols], X[:, c, 0:ncols], RS[:, 0:ncols], MUL, r=[(X, c), RS], w=[tmp])
            ts("dve", tmp[:, 0:ncols], tmp[:, 0:ncols], spcol("fnorm_w", c), MUL, r=[tmp, SP], w=[tmp])
            dma(o_y[:, c, seg * TT:(seg + 1) * TT], tmp[:, 0:TT], r=[tmp], is_out=True)
            if first:
                dma(o_y[:, c, 2048:2064], tmp[:, TT:TW], r=[tmp], is_out=True)
    mark("end")
    S.finish()
    return nc, S


def _fm(v, n):
    v = np.asarray(v, np.float32)
    lead = v.shape[:-1]
    return np.ascontiguousarray(np.moveaxis(v.reshape(lead + (n, 128)), -1, 0))


def _state_layout(a):
    L = a.shape[0]
    return np.ascontiguousarray(a.reshape(L, 32, 2, 64).transpose(2, 3, 0, 1).reshape(128, L, 32))


def kernel(**inp):
    f32 = np.float32
    g = {k: np.asarray(v) for k, v in inp.items()}
    lay, SPN = sp_layout()
    sp = np.zeros((128, SPN), f32)

    def put(name, arr):
        off, w = lay[name]
        a = np.asarray(arr, f32).reshape(128, -1)
        assert a.shape[1] == w, (name, a.shape, w)
        sp[:, off:off + w] = a

    put("mod_b", _fm(g["mod_b"], 24)); put("norm_w", _fm(g["norm_w"], 8)); put("fnorm_w", _fm(g["final_norm_w"], 8))
    put("conv_a_w", g["conv_a_w"].reshape(2, 4, 8, 128).transpose(3, 0, 2, 1))
    put("conv_a_b", _fm(g["conv_a_b"], 8)); put("gx_b", _fm(g["lru_gx_b"], 8)); put("ga_b", _fm(g["lru_ga_b"], 8))
    put("a_param", _fm(g["lru_a_param"], 8))
    put("conv_b_w", g["conv_b_w"].reshape(2, 4, 24, 128).transpose(3, 0, 2, 1))
    put("gnorm_w", g["gdn_norm_w"].T)
    put("a_log", np.broadcast_to(g["gdn_a_log"][None, :, None, :], (128, 2, 9, 8)))
    put("dt_bias", np.broadcast_to(g["gdn_dt_bias"][None, :, None, :], (128, 2, 9, 8)))
    put("s5_are", _state_layout(g["s5_a_re"])); put("s5_aim", _state_layout(g["s5_a_im"]))
    put("s5_ldt", _state_layout(np.broadcast_to(g["s5_log_dt"][:, :, None], (2, 64, 64))))
    put("s5_d", _fm(g["s5_d"], 8)); put("glu_b", _fm(g["glu_b"], 8))

    def wl(w, n):
        L, _, C = w.shape
        return np.ascontiguousarray(w.reshape(L, n, 128, C).transpose(0, 2, 1, 3).astype(f32))

    s5B = np.zeros((2, 2, 32, 128, 128), f32); s5C = np.zeros((2, 2, 32, 128, 128), f32)
    for ri, (b, c) in enumerate(((g["s5_b_re"], g["s5_c_re"]), (g["s5_b_im"], g["s5_c_im"]))):
        for gg in range(64):
            gp, g2, r0 = gg // 2, gg % 2, (gg % 8) * 16
            s5B[:, ri, gp, r0:r0 + 16, g2 * 64:(g2 + 1) * 64] = b[:, gg].transpose(0, 2, 1)
            s5C[:, ri, gp, g2 * 64:(g2 + 1) * 64, r0:r0 + 16] = c[:, gg].transpose(0, 2, 1)
    shared = {
        "sp": sp, "modw": wl(g["mod_w"], 8), "abin": wl(g["ab_in_w"], 8), "about": wl(g["ab_out_w"], 16),
        "gxw": np.ascontiguousarray(g["lru_gx_w"].transpose(0, 2, 1, 3).astype(f32)),
        "gaw": np.ascontiguousarray(g["lru_ga_w"].transpose(0, 2, 1, 3).astype(f32)),
        "cin": wl(g["c_in_w"], 8), "cout": wl(g["c_out_w"], 8), "gluw": wl(g["glu_w"], 8),
        "s5B": s5B, "s5C": s5C,
    }
    ii = np.arange(128)

    def bm(b):
        return (ii[:, None] // b == ii[None, :] // b).astype(f32)
    shared["mk"] = np.stack([bm(16), bm(32) - bm(16), bm(64) - bm(32), bm(128) - bm(64)], 1)
    in_maps = []
    for i in range(8):
        sl = slice(16 * i, 16 * i + 16)
        m = dict(shared)
        m["xp"] = _fm(g["x_prompt"][i], 8).transpose(0, 2, 1).copy()
        m["xs"] = _fm(g["x_sample"][sl, 0], 8).transpose(0, 2, 1).copy()
        cc = np.concatenate([g["c_prompt"][i:i + 1], g["c_sample"][sl]], 0)
        m["cT"] = _fm(cc, 8).transpose(0, 2, 1).copy()
        m["sca"] = g["state_conv_a"][:, sl].reshape(2, 16, 3, 8, 128).transpose(4, 0, 3, 1, 2).copy()
        m["slru"] = g["state_lru"][:, sl].reshape(2, 16, 8, 128).transpose(3, 0, 2, 1).copy()
        m["scb"] = g["state_conv_b"][:, sl].reshape(2, 16, 3, 24, 128).transpose(4, 0, 3, 1, 2).copy()
        m["sdelta"] = np.ascontiguousarray(g["state_delta"][:, sl])
        for nm, key in (("ss5re", "state_s5_re"), ("ss5im", "state_s5_im")):
            m[nm] = g[key][:, sl].reshape(2, 16, 32, 2, 64).transpose(3, 4, 0, 2, 1).reshape(128, 2, 32, 16).copy()
        in_maps.append({k: np.ascontiguousarray(v, dtype=f32) for k, v in m.items()})

    nc, _ = build_program()
    import os
    _ncores = int(os.environ.get("KCORES", "8"))
    res = run_bass_kernel_spmd(nc, in_maps[:_ncores], core_ids=list(range(_ncores))).results
    res = list(res) + [res[0]] * (8 - _ncores)

    y_p = np.zeros((8, 2048, 1024), f32); y_s = np.zeros((128, 1, 1024), f32)
    ca = [np.zeros((2, 8, 3, 1024), f32), np.zeros((2, 128, 3, 1024), f32)]
    lru = [np.zeros((2, 8, 1024), f32), np.zeros((2, 128, 1024), f32)]
    cb = [np.zeros((2, 8, 3, 3072), f32), np.zeros((2, 128, 3, 3072), f32)]
    dl = [np.zeros((2, 8, 8, 128, 128), f32), np.zeros((2, 128, 8, 128, 128), f32)]
    sr = [np.zeros((2, 8, 64, 64), f32), np.zeros((2, 128, 64, 64), f32)]
    si = [np.zeros((2, 8, 64, 64), f32), np.zeros((2, 128, 64, 64), f32)]
    for i in range(8):
        r = res[i]
        sl = slice(16 * i, 16 * i + 16)
        y = r["o_y"].transpose(2, 1, 0).reshape(2064, 1024)
        y_p[i] = y[:2048]; y_s[sl, 0] = y[2048:]
        a = r["o_ca"].transpose(1, 3, 4, 2, 0).reshape(2, 17, 3, 1024)
        ca[0][:, i] = a[:, 0]; ca[1][:, sl] = a[:, 1:]
        a = r["o_lru"].transpose(1, 3, 2, 0).reshape(2, 17, 1024)
        lru[0][:, i] = a[:, 0]; lru[1][:, sl] = a[:, 1:]
        a = r["o_cb"].transpose(1, 3, 4, 2, 0).reshape(2, 17, 3, 3072)
        cb[0][:, i] = a[:, 0]; cb[1][:, sl] = a[:, 1:]
        a = r["o_delta"]
        dl[0][:, i] = a[:, 0]; dl[1][:, sl] = a[:, 1:]
        for o_, nm, nmp in ((sr, "o_s5re", "o_s5pre"), (si, "o_s5im", "o_s5pim")):
            a = r[nm].reshape(2, 64, 2, 32, 16).transpose(2, 4, 3, 0, 1).reshape(2, 16, 64, 64)
            o_[1][:, sl] = a
            o_[0][:, i] = r[nmp].reshape(2, 64, 2, 32).transpose(2, 3, 0, 1).reshape(2, 64, 64)
    return (y_p, y_s, ca[0], lru[0], cb[0], dl[0], sr[0], si[0], ca[1], lru[1], cb[1], dl[1], sr[1], si[1])
```
